# Optimizing a Trainium2 kernel written in Bass

```python
import math
import jax, jax.numpy as jnp
from jax import lax
import numpy as np

D_MODEL = 2048
BATCH = 4
SEQ = 2048
DEPTH = 4
DEC_BATCH = 128
DEC_SEQ = 4
PAST_LEN = 16384
PAGE_SIZE = 128

N_EVEN = (DEPTH + 1) // 2
N_ODD = DEPTH // 2
D_RET = D_MODEL // 2
H_RET = 4
DK_RET = D_RET // H_RET
DV_RET = D_RET // H_RET
RET_CHUNK = 128
ROPE_BASE = 10000.0
D_RG = D_MODEL // 2
RG_BLOCKS = 8
RG_BW = D_RG // RG_BLOCKS
RG_CONV = 4
RG_C = 8.0
D_HG = D_MODEL
HG_DK = 128
H_HG = D_HG // HG_DK
HG_DV = D_HG // H_HG
HG_CHUNK = 32
D_FF = 5632
FFN_CONV = 3
DN_ALPHA = (2 * DEPTH) ** 0.25
DN_BETA = (8 * DEPTH) ** -0.25
LN_EPS = 1e-5
NORM_EPS = 1e-6

EVEN_IN = 4 * D_RET + 2 * D_RG
ODD_IN = 4 * D_HG

kernel_name = "hybrid_retention_rglru_hgrn2_step"

F32 = jnp.float32


def _layer_norm(x, g, b):
    xf = x.astype(F32)
    mu = xf.mean(-1, keepdims=True)
    var = jnp.square(xf - mu).mean(-1, keepdims=True)
    return ((xf - mu) * lax.rsqrt(var + LN_EPS) * g.astype(F32) + b.astype(F32)).astype(x.dtype)


def _rms_norm(x):
    xf = x.astype(F32)
    return xf * lax.rsqrt(jnp.mean(xf * xf, axis=-1, keepdims=True) + NORM_EPS)


def _rotary(x, pos):
    half = x.shape[-1] // 2
    inv = ROPE_BASE ** (-jnp.arange(half, dtype=F32) / half)
    ang = pos.astype(F32)[:, None] * inv[None, :]
    cos = jnp.cos(ang)[None, :, None, :]
    sin = jnp.sin(ang)[None, :, None, :]
    xf = x.astype(F32)
    x1, x2 = xf[..., :half], xf[..., half:]
    return jnp.concatenate([x1 * cos - x2 * sin, x2 * cos + x1 * sin], axis=-1)


def _causal_dwconv(x, buf, w, b):
    T = x.shape[1]
    W = w.shape[0]
    xp = jnp.concatenate([buf.astype(x.dtype), x], axis=1)
    y = xp[:, 0:T] * w[0] + b
    for j in range(1, W):
        y = y + xp[:, j:j + T] * w[j]
    return y, xp[:, xp.shape[1] - (W - 1):]


def _chunk_len(T, chunk):
    return chunk if T % chunk == 0 else T


def _to_chunks(a, N, C):
    B, T, H, d = a.shape
    return a.reshape(B, N, C, H, d).transpose(1, 0, 3, 2, 4)


def _from_chunks(o):
    N, B, H, C, d = o.shape
    return o.transpose(1, 0, 3, 2, 4).reshape(B, N * C, H, d)


def _retention(q, k, v, S0, chunk):
    B, T, H, _ = q.shape
    C = _chunk_len(T, chunk)
    N = T // C
    lg = jnp.log(1.0 - 2.0 ** (-5.0 - jnp.arange(H, dtype=F32)))
    idx = jnp.arange(C, dtype=F32)
    diff = idx[:, None] - idx[None, :]
    dmask = jnp.where(diff[None] >= 0, jnp.exp(jnp.maximum(diff, 0.0)[None] * lg[:, None, None]), 0.0)
    xi = jnp.exp((idx[None, :] + 1.0) * lg[:, None])[None, :, :, None]
    zeta = jnp.exp((C - 1.0 - idx[None, :]) * lg[:, None])[None, :, :, None]
    cdec = jnp.exp(C * lg)[None, :, None, None]

    def step(S, inp):
        qc, kc, vc = inp
        inner = jnp.einsum('bhtk,bhsk->bhts', qc, kc) * dmask[None]
        o = jnp.einsum('bhts,bhsv->bhtv', inner, vc) + jnp.einsum('bhtk,bhkv->bhtv', qc, S) * xi
        S = S * cdec + jnp.einsum('bhsk,bhsv->bhkv', kc * zeta, vc)
        return S, o

    S, o = lax.scan(step, S0, (_to_chunks(q, N, C), _to_chunks(k, N, C), _to_chunks(v, N, C)))
    return _from_chunks(o), S


def _gla(q, k, v, logf, S0, chunk):
    B, T, H, _ = q.shape
    C = _chunk_len(T, chunk)
    N = T // C
    causal = jnp.tril(jnp.ones((C, C), dtype=bool))

    def step(S, inp):
        qc, kc, vc, gc = inp
        b = jnp.cumsum(gc, axis=2)
        diff = b[:, :, :, None, :] - b[:, :, None, :, :]
        decay = jnp.exp(jnp.where(causal[None, None, :, :, None], diff, -jnp.inf))
        A = jnp.einsum('bhtk,bhsk,bhtsk->bhts', qc, kc, decay)
        o = jnp.einsum('bhts,bhsv->bhtv', A, vc) + jnp.einsum('bhtk,bhkv->bhtv', qc * jnp.exp(b), S)
        b_last = b[:, :, -1:, :]
        S = S * jnp.exp(b_last[:, :, 0, :])[..., None] + jnp.einsum('bhsk,bhsv->bhkv', kc * jnp.exp(b_last - b), vc)
        return S, o

    S, o = lax.scan(step, S0, (_to_chunks(q, N, C), _to_chunks(k, N, C), _to_chunks(v, N, C), _to_chunks(logf, N, C)))
    return _from_chunks(o), S


def _rglru(xc, pos, h0, wa, ba, wx, bx, lam):
    B, T, D = xc.shape
    xb = xc.reshape(B, T, RG_BLOCKS, RG_BW)
    r = jax.nn.sigmoid((jnp.einsum('btni,nij->btnj', xb, wa).reshape(B, T, D) + ba).astype(F32))
    i = jax.nn.sigmoid((jnp.einsum('btni,nij->btnj', xb, wx).reshape(B, T, D) + bx).astype(F32))
    log_a = -RG_C * r * jax.nn.softplus(-lam.astype(F32))
    a = jnp.exp(log_a)
    mult = jnp.sqrt(-jnp.expm1(2.0 * log_a))
    mult = jnp.where((pos == 0)[None, :, None], 1.0, mult)
    bterm = xc.astype(F32) * i * mult
    bterm = bterm.at[:, 0].add(a[:, 0] * h0.astype(F32))

    def comb(l, rr):
        return (l[0] * rr[0], rr[0] * l[1] + rr[1])

    _, h = lax.associative_scan(comb, (a, bterm), axis=1)
    return h, h[:, -1]


def _even_mixer(x, pos, S_ret, h_rg, buf_rg, w_in, w_out, cw, cb, wa, ba, wx, bx, lam):
    B, T, _ = x.shape
    dt = x.dtype
    q, k, v, g, xr, gr = jnp.split(x @ w_in, [D_RET, 2 * D_RET, 3 * D_RET, 4 * D_RET, 4 * D_RET + D_RG], axis=-1)
    q = _rotary(q.reshape(B, T, H_RET, DK_RET), pos)
    k = _rotary(k.reshape(B, T, H_RET, DK_RET), pos) * (DK_RET ** -0.5)
    v = v.reshape(B, T, H_RET, DV_RET).astype(F32)
    o, S_new = _retention(q, k, v, S_ret.astype(F32), RET_CHUNK)
    o_ret = _rms_norm(o).reshape(B, T, D_RET) * jax.nn.silu(g.astype(F32))
    xc, buf_new = _causal_dwconv(xr, buf_rg, cw, cb)
    h, h_last = _rglru(xc, pos, h_rg, wa, ba, wx, bx, lam)
    o_rg = h * jax.nn.gelu(gr.astype(F32))
    mixed = jnp.concatenate([o_ret, o_rg], axis=-1).astype(dt)
    return (mixed @ w_out, S_new.astype(S_ret.dtype), h_last.astype(h_rg.dtype), buf_new.astype(buf_rg.dtype))


def _odd_mixer(x, S_hg, w_in, w_out, norm_g, lb):
    B, T, _ = x.shape
    dt = x.dtype
    q, f, i, g = jnp.split(x @ w_in, 4, axis=-1)
    q = jax.nn.silu(q.astype(F32))
    fg = lb.astype(F32) + (1.0 - lb.astype(F32)) * jax.nn.sigmoid(f.astype(F32))
    logf = jnp.log(fg)
    kk = 1.0 - fg
    o, S_new = _gla(q.reshape(B, T, H_HG, HG_DK), kk.reshape(B, T, H_HG, HG_DK),
                    i.astype(F32).reshape(B, T, H_HG, HG_DV), logf.reshape(B, T, H_HG, HG_DK),
                    S_hg.astype(F32), HG_CHUNK)
    o = (_rms_norm(o) * norm_g.astype(F32)).reshape(B, T, D_HG) * jax.nn.sigmoid(g.astype(F32))
    return o.astype(dt) @ w_out, S_new.astype(S_hg.dtype)


def _conv_ffn(x, buf, w_up, cw, cb, w_down):
    u, v = jnp.split(x @ w_up, 2, axis=-1)
    uc, buf_new = _causal_dwconv(u, buf, cw, cb)
    return (jax.nn.gelu(uc) * v) @ w_down, buf_new.astype(buf.dtype)


def _trunk(x, pos, s_ret, s_h, s_conv, s_hg, s_ffc, w, lb):
    n_ret, n_h, n_conv, n_hg, n_ffc = [], [], [], [], []
    for l in range(DEPTH):
        if l % 2 == 0:
            e = l // 2
            mix, sr, sh, sc = _even_mixer(x, pos, s_ret[e], s_h[e], s_conv[e], w['ev_w_in'][e], w['ev_w_out'][e],
                                          w['ev_rg_conv_w'][e], w['ev_rg_conv_b'][e], w['ev_rg_wa'][e], w['ev_rg_ba'][e],
                                          w['ev_rg_wx'][e], w['ev_rg_bx'][e], w['ev_rg_lambda'][e])
            n_ret.append(sr); n_h.append(sh); n_conv.append(sc)
        else:
            o = l // 2
            mix, sg = _odd_mixer(x, s_hg[o], w['od_w_in'][o], w['od_w_out'][o], w['od_norm_g'][o], lb[o])
            n_hg.append(sg)
        x = _layer_norm(DN_ALPHA * x + mix, w['ln_g'][l, 0], w['ln_b'][l, 0])
        f, fc = _conv_ffn(x, s_ffc[l], w['ffn_w_up'][l], w['ffn_conv_w'][l], w['ffn_conv_b'][l], w['ffn_w_down'][l])
        n_ffc.append(fc)
        x = _layer_norm(DN_ALPHA * x + f, w['ln_g'][l, 1], w['ln_b'][l, 1])
    return x, (jnp.stack(n_ret), jnp.stack(n_h), jnp.stack(n_conv), jnp.stack(n_hg), jnp.stack(n_ffc))


def setup_inputs(seed: int = 0) -> dict:
    key = jax.random.key(seed)
    ks = jax.random.split(key, 32)
    nrm = lambda k, s, sc: jax.random.normal(k, s, F32) * sc
    a0 = jax.random.uniform(ks[13], (N_EVEN, D_RG), F32, minval=0.9, maxval=0.999)
    s0 = a0 ** (1.0 / RG_C)
    lam = jnp.log(s0) - jnp.log1p(-s0)
    return {
        'x_prompt': nrm(ks[0], (BATCH, SEQ, D_MODEL), 1.0),
        'x_sample': nrm(ks[1], (DEC_BATCH, DEC_SEQ, D_MODEL), 1.0),
        'state_ret': nrm(ks[2], (N_EVEN, DEC_BATCH, H_RET, DK_RET, DV_RET), 0.1),
        'state_rglru_h': nrm(ks[3], (N_EVEN, DEC_BATCH, D_RG), 0.5),
        'state_rglru_conv': nrm(ks[4], (N_EVEN, DEC_BATCH, RG_CONV - 1, D_RG), 1.0),
        'state_hgrn': nrm(ks[5], (N_ODD, DEC_BATCH, H_HG, HG_DK, HG_DV), 0.1),
        'state_ffn_conv': nrm(ks[6], (DEPTH, DEC_BATCH, FFN_CONV - 1, D_FF), 1.0),
        'ev_w_in': nrm(ks[7], (N_EVEN, D_MODEL, EVEN_IN), D_MODEL ** -0.5),
        'ev_w_out': nrm(ks[8], (N_EVEN, D_RET + D_RG, D_MODEL), (D_RET + D_RG) ** -0.5 * DN_BETA),
        'ev_rg_conv_w': nrm(ks[9], (N_EVEN, RG_CONV, D_RG), RG_CONV ** -0.5),
        'ev_rg_conv_b': nrm(ks[10], (N_EVEN, D_RG), 0.01),
        'ev_rg_wa': nrm(ks[11], (N_EVEN, RG_BLOCKS, RG_BW, RG_BW), RG_BW ** -0.5),
        'ev_rg_ba': nrm(ks[12], (N_EVEN, D_RG), 0.01),
        'ev_rg_wx': nrm(ks[14], (N_EVEN, RG_BLOCKS, RG_BW, RG_BW), RG_BW ** -0.5),
        'ev_rg_bx': nrm(ks[15], (N_EVEN, D_RG), 0.01),
        'ev_rg_lambda': lam,
        'od_w_in': nrm(ks[16], (N_ODD, D_MODEL, ODD_IN), D_MODEL ** -0.5),
        'od_w_out': nrm(ks[17], (N_ODD, D_HG, D_MODEL), D_HG ** -0.5 * DN_BETA),
        'od_norm_g': 1.0 + nrm(ks[18], (N_ODD, HG_DV), 0.01),
        'od_lb_logits': nrm(ks[19], (N_ODD, D_HG), 0.1),
        'ln_g': 1.0 + nrm(ks[20], (DEPTH, 2, D_MODEL), 0.01),
        'ln_b': nrm(ks[21], (DEPTH, 2, D_MODEL), 0.01),
        'ffn_w_up': nrm(ks[22], (DEPTH, D_MODEL, 2 * D_FF), D_MODEL ** -0.5),
        'ffn_conv_w': nrm(ks[23], (DEPTH, FFN_CONV, D_FF), FFN_CONV ** -0.5),
        'ffn_conv_b': nrm(ks[24], (DEPTH, D_FF), 0.01),
        'ffn_w_down': nrm(ks[25], (DEPTH, D_FF, D_MODEL), D_FF ** -0.5 * DN_BETA),
    }


def reference(x_prompt, x_sample, state_ret, state_rglru_h, state_rglru_conv, state_hgrn, state_ffn_conv,
              ev_w_in, ev_w_out, ev_rg_conv_w, ev_rg_conv_b, ev_rg_wa, ev_rg_ba, ev_rg_wx, ev_rg_bx, ev_rg_lambda,
              od_w_in, od_w_out, od_norm_g, od_lb_logits, ln_g, ln_b, ffn_w_up, ffn_conv_w, ffn_conv_b, ffn_w_down):
    w = {'ev_w_in': ev_w_in, 'ev_w_out': ev_w_out, 'ev_rg_conv_w': ev_rg_conv_w, 'ev_rg_conv_b': ev_rg_conv_b,
         'ev_rg_wa': ev_rg_wa, 'ev_rg_ba': ev_rg_ba, 'ev_rg_wx': ev_rg_wx, 'ev_rg_bx': ev_rg_bx,
         'ev_rg_lambda': ev_rg_lambda, 'od_w_in': od_w_in, 'od_w_out': od_w_out, 'od_norm_g': od_norm_g,
         'ln_g': ln_g, 'ln_b': ln_b, 'ffn_w_up': ffn_w_up, 'ffn_conv_w': ffn_conv_w,
         'ffn_conv_b': ffn_conv_b, 'ffn_w_down': ffn_w_down}
    p = jax.nn.softmax(od_lb_logits.astype(F32), axis=0)
    lb = jnp.cumsum(p, axis=0) - p[0:1]

    Bp, Tp, _ = x_prompt.shape
    dt = x_prompt.dtype
    pos_p = jnp.arange(Tp, dtype=jnp.int32)
    pos_s = PAST_LEN + jnp.arange(x_sample.shape[1], dtype=jnp.int32)
    z_ret = jnp.zeros((N_EVEN, Bp, H_RET, DK_RET, DV_RET), dt)
    z_h = jnp.zeros((N_EVEN, Bp, D_RG), dt)
    z_conv = jnp.zeros((N_EVEN, Bp, RG_CONV - 1, D_RG), dt)
    z_hg = jnp.zeros((N_ODD, Bp, H_HG, HG_DK, HG_DV), dt)
    z_ffc = jnp.zeros((DEPTH, Bp, FFN_CONV - 1, D_FF), dt)

    y_prompt, (rp, hp, cp, gp, fp) = _trunk(x_prompt, pos_p, z_ret, z_h, z_conv, z_hg, z_ffc, w, lb)
    y_sample, (rs, hs, cs, gs, fs) = _trunk(x_sample, pos_s, state_ret, state_rglru_h, state_rglru_conv,
                                            state_hgrn, state_ffn_conv, w, lb)
    return (y_prompt, y_sample, rp, rs, hp, hs, cp, cs, gp, gs, fp, fs)
```

```python
import numpy as np
import concourse.bass as bass
import concourse.mybir as mybir
from concourse.bass_utils import run_bass_kernel_spmd
from contextlib import ExitStack

F32 = mybir.dt.float32
BF16 = mybir.dt.bfloat16
AF = mybir.ActivationFunctionType
ALU = mybir.AluOpType

D_MODEL = 2048
DEPTH = 4
SEQ = 2048
DEC_BATCH = 128
DEC_SEQ = 4
PAST_LEN = 16384
D_FF = 5632
NFC = D_FF // 128
ALPHA = float((2 * DEPTH) ** 0.25)
LN_EPS = 1e-5
NORM_EPS = 1e-6
NS = 16
TS = NS * DEC_SEQ

CT_ID = 0
CT_ONES = 128
CT_DMP = 256
CT_BCM = 768
CT_CM = 896
CT_ZP = 900
CT_DMS = 904
CT_BCS = 1160
CT_SQM = 1224
CT_ZS = 1240
NCT = 1244


def _host_tables():
    f32 = np.float32
    half = 128
    inv = np.power(f32(10000.0), -np.arange(half, dtype=f32) / f32(half)).astype(f32)
    pos = np.concatenate([np.arange(SEQ), np.tile(PAST_LEN + np.arange(DEC_SEQ), NS)]).astype(f32)
    ang = (pos[:, None] * inv[None, :]).astype(f32)
    cos = np.cos(ang).astype(f32).T
    sin = np.sin(ang).astype(f32).T
    lg = np.log(f32(1.0) - np.power(f32(2.0), -5.0 - np.arange(4, dtype=f32))).astype(f32)
    idx_p = (np.arange(SEQ) % 128).astype(f32)
    idx_s = np.tile(np.arange(DEC_SEQ), NS).astype(f32)
    idx = np.concatenate([idx_p, idx_s])
    rot = np.zeros((10, 128, SEQ + TS), f32)
    ksc = f32(256.0 ** -0.5)
    rot[0] = cos * ksc
    rot[1] = sin * ksc
    for h in range(4):
        xi = np.exp((idx + 1.0) * lg[h]).astype(f32)
        rot[2 + 2 * h] = cos * xi[None, :]
        rot[3 + 2 * h] = sin * xi[None, :]
    ct = np.zeros((128, NCT), f32)
    ct[:, CT_ID:CT_ID + 128] = np.eye(128, dtype=f32)
    ct[:, CT_ONES:CT_ONES + 128] = 1.0
    s = np.arange(128)
    t = np.arange(128)
    for h in range(4):
        m = np.where(t[None, :] >= s[:, None], np.exp(-(s[:, None] + 1.0) * lg[h]), 0.0)
        ct[:, CT_DMP + h * 128:CT_DMP + (h + 1) * 128] = m
        ct[:, CT_ZP + h] = np.exp((127.0 - s) * lg[h])
    ct[:, CT_BCM:CT_BCM + 128] = ((s[:, None] // 32 == t[None, :] // 32) & (t[None, :] >= s[:, None])).astype(f32)
    for j in range(4):
        ct[:, CT_CM + j] = (s // 32 == j).astype(f32)
    s = np.arange(64)
    t = np.arange(64)
    same = (s[:, None] // 4 == t[None, :] // 4) & (t[None, :] >= s[:, None])
    for h in range(4):
        m = np.where(same, np.exp(-((s[:, None] % 4) + 1.0) * lg[h]), 0.0)
        ct[:64, CT_DMS + h * 64:CT_DMS + (h + 1) * 64] = m
        ct[:64, CT_ZS + h] = np.exp((3.0 - (s % 4)) * lg[h])
    ct[:64, CT_BCS:CT_BCS + 64] = same.astype(f32)
    for b in range(16):
        ct[:64, CT_SQM + b] = (s // 4 == b).astype(f32)
    cdec_p = [float(np.exp(f32(128.0) * lg[h])) for h in range(4)]
    cdec_s = [float(np.exp(f32(4.0) * lg[h])) for h in range(4)]
    return rot, ct, cdec_p, cdec_s


class Buf:
    __slots__ = ("name", "lw", "rd", "excl")

    def __init__(self, name, ghost=(), excl=False):
        self.name = name
        self.lw = []
        self.rd = list(ghost)
        self.excl = excl


class Eng:
    def __init__(self, name, h, sem):
        self.name = name
        self.h = h
        self.sem = sem
        self.count = 0
        self.known = {}


class K:
    def __init__(self, nc, es, n_dma_sems=10):
        self.nc = nc
        self.eng = {}
        for name, h in (("pe", nc.tensor), ("act", nc.scalar), ("dve", nc.vector),
                        ("pool", nc.gpsimd), ("sp", nc.sync)):
            s = es.enter_context(nc.semaphore("s_" + name))
            self.eng[name] = Eng(name, h, s)
        self.dma_ring = {}
        for q in ("sp", "pool"):
            ring = []
            for i in range(n_dma_sems):
                s = es.enter_context(nc.semaphore(f"d_{q}{i}"))
                ring.append([s, 0])
            self.dma_ring[q] = [ring, 0]
        self.n_wait = 0
        self.n_inst = 0
        self.ghost = {}

    def _wait(self, e, ev):
        sem, val, snap = ev
        if e.known.get(id(sem), 0) >= val:
            return
        e.h.wait_ge(sem, val)
        self.n_wait += 1
        for kk, v in snap.items():
            if e.known.get(kk, 0) < v:
                e.known[kk] = v
        e.known[id(sem)] = val

    def _deps(self, e, reads, writes, append):
        for b in reads:
            for ev in b.lw:
                self._wait(e, ev)
            if b.excl:
                for ev in b.rd:
                    if ev[0] is not e.sem:
                        self._wait(e, ev)
        for b in writes:
            if not append:
                for ev in b.lw:
                    self._wait(e, ev)
            for ev in b.rd:
                self._wait(e, ev)

    def _commit(self, ev, reads, writes, append):
        for b in reads:
            b.rd.append(ev)
        for b in writes:
            if append:
                b.lw.append(ev)
            else:
                b.lw = [ev]
            b.rd = []

    def op(self, en, fn, reads=(), writes=()):
        e = self.eng[en]
        self._deps(e, reads, writes, False)
        ins = fn(e.h)
        e.count += 1
        ins.then_inc(e.sem, 1)
        self.n_inst += 1
        ev = (e.sem, e.count, dict(e.known))
        self._commit(ev, reads, writes, False)
        return ev

    def mm(self, fns, reads=(), writes=()):
        e = self.eng["pe"]
        self._deps(e, reads, writes, False)
        ins = None
        for fn in fns:
            ins = fn(e.h)
            self.n_inst += 1
        e.count += 1
        ins.then_inc(e.sem, 1)
        ev = (e.sem, e.count, dict(e.known))
        self._commit(ev, reads, writes, False)
        return ev

    def dma(self, q, out, in_, reads=(), writes=(), append=False):
        e = self.eng[q]
        ring, pos = self.dma_ring[q]
        slot = ring[pos % len(ring)]
        self.dma_ring[q][1] = pos + 1
        sem, val = slot
        if val > 0:
            self._wait(e, (sem, val, {}))
        self._deps(e, reads, writes, append)
        e.h.dma_start(out=out, in_=in_).then_inc(sem, 16)
        self.n_inst += 1
        slot[1] = val + 16
        ev = (sem, val + 16, dict(e.known))
        self._commit(ev, reads, writes, append)
        return ev

    def retire(self, bufs):
        for b in bufs:
            for ev in list(b.lw) + list(b.rd):
                sem, val, _ = ev
                cur = self.ghost.get(id(sem))
                if cur is None or cur[1] < val:
                    self.ghost[id(sem)] = (sem, val, {})

    def ghost_events(self):
        return list(self.ghost.values())

    def finish(self, en="sp"):
        e = self.eng[en]
        for q, (ring, pos) in self.dma_ring.items():
            for sem, val in ring:
                if val > 0:
                    self._wait(e, (sem, val, {}))
        for name, o in self.eng.items():
            if o.count > 0 and name != en:
                self._wait(e, (o.sem, o.count, {}))


class Tile:
    def __init__(self, kind, idx):
        self.kind = kind
        self.idx = idx
        if kind == "p":
            self.T, self.G, self.np, self.C = 512, 4, 128, 32
            self.rc0 = idx * 512
        else:
            self.T, self.G, self.np, self.C = TS, 1, TS, 4
            self.rc0 = SEQ


class Cfg:
    def __init__(self, nl=4, tiles=("p0", "p1", "p2", "p3", "s"), dbg=None):
        self.nl = nl
        self.tiles = tuple(tiles)
        self.dbg = dbg

    def on(self, name):
        return self.dbg is None or name in self.dbg

    def key(self):
        return (self.nl, self.tiles, None if self.dbg is None else tuple(sorted(self.dbg)))


def build_program(cfg, cdec_p, cdec_s):
    nc = bass.Bass("TRN2", target_bir_lowering=False)
    es = ExitStack()

    def din(name, shape):
        return nc.dram_tensor(name, list(shape), F32, kind="ExternalInput").ap()

    def dout(name, shape):
        return nc.dram_tensor(name, list(shape), F32, kind="ExternalOutput").ap()

    xp = din("xp", [SEQ, D_MODEL])
    xs = din("xs", [TS, D_MODEL])
    s_ret = din("s_ret", [2, NS, 4, 256, 256])
    s_rgh = din("s_rgh", [2, NS, 1024])
    s_rgc = din("s_rgc", [2, NS * 3, 1024])
    s_hg = din("s_hg", [2, NS, 16, 128, 128])
    s_ffc = din("s_ffc", [4, NS * 2, D_FF])
    W = {
        "ev_w_in": din("ev_w_in", [2, D_MODEL, 6144]),
        "ev_w_out": din("ev_w_out", [2, D_MODEL, D_MODEL]),
        "ev_rg_wa": din("ev_rg_wa", [2, 8, 128, 128]),
        "ev_rg_wx": din("ev_rg_wx", [2, 8, 128, 128]),
        "od_w_in": din("od_w_in", [2, D_MODEL, 8192]),
        "od_w_out": din("od_w_out", [2, D_MODEL, D_MODEL]),
        "ffn_w_up": din("ffn_w_up", [4, D_MODEL, 2 * D_FF]),
        "ffn_w_down": din("ffn_w_down", [4, D_FF, D_MODEL]),
    }
    pv_even = din("pv_even", [2, 8, 1024])
    pv_ln = din("pv_ln", [16, D_MODEL])
    pv_ffn = din("pv_ffn", [16, D_FF])
    pv_od = din("pv_od", [4, D_MODEL])
    rot = din("rot", [10, 128, SEQ + TS])
    ctab = din("ctab", [128, NCT])

    y_p = dout("y_p", [SEQ, D_MODEL])
    y_s = dout("y_s", [TS, D_MODEL])
    ret_p = dout("ret_p", [2, 4, 256, 256])
    ret_s = dout("ret_s", [2, NS, 4, 256, 256])
    rgh_p = dout("rgh_p", [2, 1024])
    rgh_s = dout("rgh_s", [2, NS, 1024])
    rgc_p = dout("rgc_p", [2, 3, 1024])
    rgc_s = dout("rgc_s", [2, NS * 3, 1024])
    hg_p = dout("hg_p", [2, 16, 128, 128])
    hg_s = dout("hg_s", [2, NS, 16, 128, 128])
    ffc_p = dout("ffc_p", [4, 2, D_FF])
    ffc_s = dout("ffc_s", [4, NS * 2, D_FF])

    k = K(nc, es)
    uid = [0]

    def sb_raw(stack, name, shape, dt):
        uid[0] += 1
        return stack.enter_context(nc.sbuf_tensor(f"{name}_{uid[0]}", list(shape), dt))

    class Phase:
        def __init__(self):
            self.stack = ExitStack()
            self.bufs = []

        def sb(self, name, shape, dt, nb=1):
            t = sb_raw(self.stack, name, shape, dt)
            g = k.ghost_events()
            bs = [Buf(name, g) for _ in range(nb)]
            self.bufs.extend(bs)
            return (t, bs[0]) if nb == 1 else (t, bs)

        def __enter__(self):
            return self

        def __exit__(self, *a):
            k.retire(self.bufs)
            self.stack.close()
            return False

    def A(fn, reads=(), writes=()):
        return k.op("act", fn, reads, writes)

    def V(fn, reads=(), writes=()):
        return k.op("dve", fn, reads, writes)

    def MM(fns, reads=(), writes=()):
        return k.mm(fns, reads, writes)

    def Dq(q, out, in_, reads=(), writes=(), append=False):
        return k.dma(q, out, in_, reads, writes, append)

    def mmf(out, lhsT, rhs, start, stop):
        return lambda h: h.matmul(out, lhsT=lhsT, rhs=rhs, start=start, stop=stop, skip_group_check=True)

    def psb(name, shape, dt):
        return sb_raw(es, name, shape, dt), Buf(name)

    X = sb_raw(es, "X", [128, 16, 512], F32)
    Xb = [Buf(f"X{c}") for c in range(16)]
    XB = sb_raw(es, "XB", [128, 16, 512], BF16)
    XBb = [Buf(f"XB{c}") for c in range(16)]
    NSLOT = 3
    WS = [sb_raw(es, f"WS{i}", [128, 16, 512], BF16) for i in range(NSLOT)]
    WSb = [Buf(f"WS{i}") for i in range(NSLOT)]
    wpos = [0]
    RS = [sb_raw(es, f"RS{e}", [128, 4, 512], F32) for e in range(2)]
    RSb = [[Buf(f"RS{e}_{h}") for h in range(4)] for e in range(2)]
    HS = [sb_raw(es, f"HS{o}", [128, 16, 128], F32) for o in range(2)]
    HSb = [[Buf(f"HS{o}_{h}") for h in range(16)] for o in range(2)]
    RGH = [psb(f"RGH{e}", [128, 8], F32) for e in range(2)]
    RGT = [psb(f"RGT{e}", [128, 8, 3], F32) for e in range(2)]
    FTL = [psb(f"FTL{l}", [128, NFC, 2], F32) for l in range(4)]
    CT, CTb = psb("CT", [128, NCT], F32)
    IDB, IDBb = psb("IDB", [128, 128], BF16)
    ONESB, ONESBb = psb("ONESB", [128, 128], BF16)
    PVE, PVEb = psb("PVE", [128, 2, 8, 8], F32)
    PVL, PVLb = psb("PVL", [128, 16, 16], F32)
    PVF, PVFb = psb("PVF", [128, NFC, 16], F32)
    PVO, PVOb = psb("PVO", [128, 16, 4], F32)
    LBT, LBTb = psb("LBT", [128, 2, 3, 16], F32)
    RMP, RMPb = psb("RMP", [128, 512], F32)
    RMS, RMSb = psb("RMS", [128, TS], F32)
    C8, C8b = psb("C8", [128, 2, 2, 8], F32)

    IDENT = CT[:, CT_ID:CT_ID + 128]
    ONES = CT[:, CT_ONES:CT_ONES + 128]

    PSB = [es.enter_context(nc.psum_tensor(f"ps{i}", [128, 512], F32)) for i in range(8)]
    PSb = [Buf(f"ps{i}", excl=True) for i in range(8)]
    ps_free_list = list(range(8))

    class Bank:
        def __init__(self, i):
            self.i = i
            self.t = PSB[i]
            self.b = PSb[i]

    def ps_alloc():
        i = ps_free_list.pop(0)
        return Bank(i)

    def ps_free(bk):
        ps_free_list.append(bk.i)

    WC_TOTAL = (D_MODEL * 8192 + D_MODEL * D_MODEL + D_MODEL * 2 * D_FF + D_FF * D_MODEL) // 128
    WCL = [nc.dram_tensor(f"wcache{l}", [128, WC_TOTAL], BF16).ap() for l in range(cfg.nl)]

    def wk_layer(wk):
        if wk.startswith("evin"):
            return 2 * int(wk[4:])
        if wk.startswith("odin"):
            return 2 * int(wk[4:]) + 1
        return int(wk[-1])

    wcache = {}
    wc_off = [0, 0, 0, 0]
    use_cache = len(cfg.tiles) > 1
    cur_tile = [0]

    def wload(wk, pieces, k0, kn):
        i = wpos[0] % NSLOT
        wpos[0] += 1
        ncols = sum(n for _, _, n in pieces)
        key = (wk, tuple((c0, n) for _, c0, n in pieces), k0)
        lyr = wk_layer(wk)
        WC = WCL[lyr]
        if use_cache and key in wcache:
            off0, cbuf = wcache[key]
            Dq("pool", WS[i][:, 0:kn, 0:ncols], WC[:, off0:off0 + kn * ncols].rearrange("p (c n) -> p c n", n=ncols),
               reads=[cbuf], writes=[WSb[i]])
            return WS[i], WSb[i]
        off = 0
        first = True
        for (W2, c0, n) in pieces:
            src = W2[k0 * 128:(k0 + kn) * 128, c0:c0 + n].rearrange("(c p) n -> p c n", p=128)
            Dq("pool", WS[i][:, 0:kn, off:off + n], src, writes=[WSb[i]], append=not first)
            first = False
            off += n
        if use_cache and ((cur_tile[0] == 0 and lyr < 2) or cur_tile[0] >= 1):
            cbuf = Buf("wc")
            off0 = wc_off[lyr]
            wc_off[lyr] += kn * ncols
            assert wc_off[lyr] <= WC_TOTAL
            Dq("sp", WC[:, off0:off0 + kn * ncols].rearrange("p (c n) -> p c n", n=ncols), WS[i][:, 0:kn, 0:ncols],
               reads=[WSb[i]], writes=[cbuf])
            wcache[key] = (off0, cbuf)
        return WS[i], WSb[i]

    def linear_fm(wk, W2, blocks, rhs, T, evac):
        KC = len(rhs)
        kgroups = [(k0, min(16, KC - k0)) for k0 in range(0, KC, 16)]
        for bi, blk in enumerate(blocks):
            nch = sum(n for _, n in blk) // 128
            banks = [ps_alloc() for _ in range(nch)]
            for (k0, kn) in kgroups:
                slot, sbuf = wload(wk, [(W2, c0, n) for c0, n in blk], k0, kn)
                for j in range(nch):
                    fns = [mmf(banks[j].t[:, 0:T], slot[:, kc, j * 128:(j + 1) * 128], rhs[k0 + kc][0],
                               (k0 + kc == 0), (k0 + kc == KC - 1)) for kc in range(kn)]
                    MM(fns, reads=[sbuf] + [rhs[k0 + kc][1] for kc in range(kn)], writes=[banks[j].b])
            evac(bi, banks)
            for bk in banks:
                ps_free(bk)

    xb_rhs = lambda T: [(XB[:, c, 0:T], XBb[c]) for c in range(16)]

    def setup():
        Dq("sp", CT[:], ctab, writes=[CTb])
        V(lambda h: h.tensor_copy(out=IDB[:], in_=IDENT), reads=[CTb], writes=[IDBb])
        V(lambda h: h.memset(ONESB[:], 1.0), writes=[ONESBb])
        for e in range(2):
            V(lambda h, e=e: h.memset(RS[e][:], 0.0), writes=RSb[e])
            V(lambda h, e=e: h.memset(HS[e][:], 0.0), writes=HSb[e])
            V(lambda h, e=e: h.memset(RGH[e][0][:], 0.0), writes=[RGH[e][1]])
            V(lambda h, e=e: h.memset(RGT[e][0][:], 0.0), writes=[RGT[e][1]])
        for l in range(4):
            V(lambda h, l=l: h.memset(FTL[l][0][:], 0.0), writes=[FTL[l][1]])
        V(lambda h: h.memset(RMP[:], 1.0), writes=[RMPb])
        V(lambda h: h.memset(RMP[:].rearrange("p (n j) -> p n j", j=32)[:, :, 0:1], 0.0), writes=[RMPb])
        V(lambda h: h.memset(RMS[:], 1.0), writes=[RMSb])
        V(lambda h: h.memset(RMS[:].rearrange("p (n j) -> p n j", j=4)[:, :, 0:1], 0.0), writes=[RMSb])

        def rows_to_fm(src2d, R, nchunks, dst_fn, dstb):
            with Phase() as ph:
                rows, rowsb = ph.sb("rows", [R, nchunks * 128], F32)
                Dq("sp", rows[:], src2d, writes=[rowsb])
                per = 512 // R
                c = 0
                while c < nchunks:
                    n = min(per, nchunks - c)
                    bk = ps_alloc()
                    fns = [mmf(bk.t[:, j * R:(j + 1) * R], rows[0:R, (c + j) * 128:(c + j + 1) * 128], IDENT[0:R, 0:R],
                               j == 0, j == n - 1) for j in range(n)]
                    MM(fns, reads=[rowsb, CTb], writes=[bk.b])
                    A(lambda h, bk=bk, c=c, n=n: h.activation(out=dst_fn(c, n), in_=bk.t[:, 0:n * R].rearrange("p (j r) -> p j r", r=R), func=AF.Copy),
                      reads=[bk.b], writes=[dstb])
                    ps_free(bk)
                    c += n

        for e in range(2):
            rows_to_fm(pv_even[e], 8, 8, lambda c, n, e=e: PVE[:, e, c:c + n, :], PVEb)
        rows_to_fm(pv_ln, 16, 16, lambda c, n: PVL[:, c:c + n, :], PVLb)
        rows_to_fm(pv_ffn, 16, NFC, lambda c, n: PVF[:, c:c + n, :], PVFb)
        rows_to_fm(pv_od, 4, 16, lambda c, n: PVO[:, c:c + n, :], PVOb)
        V(lambda h: h.memset(LBT[:, 0, 0, :], 0.0), writes=[LBTb])
        V(lambda h: h.memset(LBT[:, 0, 1, :], 1.0), writes=[LBTb])
        V(lambda h: h.memset(LBT[:, 0, 2, :], -1.0), writes=[LBTb])
        V(lambda h: h.tensor_tensor(out=LBT[:, 1, 0, :], in0=PVO[:, :, 1], in1=PVO[:, :, 0], op=ALU.subtract), reads=[PVOb], writes=[LBTb])
        A(lambda h: h.activation(out=LBT[:, 1, 0, :], in_=LBT[:, 1, 0, :], func=AF.Sigmoid), reads=[LBTb], writes=[LBTb])
        V(lambda h: h.tensor_scalar(out=LBT[:, 1, 1, :], in0=LBT[:, 1, 0, :], scalar1=-1.0, scalar2=1.0, op0=ALU.mult, op1=ALU.add), reads=[LBTb], writes=[LBTb])
        V(lambda h: h.tensor_scalar(out=LBT[:, 1, 2, :], in0=LBT[:, 1, 0, :], scalar1=1.0, scalar2=-1.0, op0=ALU.mult, op1=ALU.add), reads=[LBTb], writes=[LBTb])
        for e in range(2):
            A(lambda h, e=e: h.activation(out=C8[:, e, 0, :], in_=PVE[:, e, :, 7], func=AF.Exp, scale=-1.0), reads=[PVEb], writes=[C8b])
            A(lambda h, e=e: h.activation(out=C8[:, e, 0, :], in_=C8[:, e, 0, :], func=AF.Ln, bias=1.0), reads=[C8b], writes=[C8b])
            V(lambda h, e=e: h.tensor_scalar(out=C8[:, e, 1, :], in0=C8[:, e, 0, :], scalar1=-16.0, scalar2=None, op0=ALU.mult), reads=[C8b], writes=[C8b])
            V(lambda h, e=e: h.tensor_scalar(out=C8[:, e, 0, :], in0=C8[:, e, 0, :], scalar1=-8.0, scalar2=None, op0=ALU.mult), reads=[C8b], writes=[C8b])

    def load_x(tl):
        T, G, npp = tl.T, tl.G, tl.np
        with Phase() as ph:
            XT, XTb = ph.sb("XT", [128, G, D_MODEL], F32)
            if tl.kind == "p":
                Dq("sp", XT[:], xp[tl.idx * 512:(tl.idx + 1) * 512, :].rearrange("(g p) d -> p g d", p=128), writes=[XTb])
            else:
                Dq("sp", XT[0:npp, 0, :], xs, writes=[XTb])
            for c in range(16):
                bk = ps_alloc()
                fns = [mmf(bk.t[:, g * npp:(g + 1) * npp], XT[0:npp, g, c * 128:(c + 1) * 128], IDENT[0:npp, 0:npp], g == 0, g == G - 1)
                       for g in range(G)]
                MM(fns, reads=[XTb, CTb], writes=[bk.b])
                A(lambda h, bk=bk, c=c: h.activation(out=X[:, c, 0:T], in_=bk.t[:, 0:T], func=AF.Copy), reads=[bk.b], writes=[Xb[c]])
                V(lambda h, bk=bk, c=c: h.tensor_copy(out=XB[:, c, 0:T], in_=bk.t[:, 0:T]), reads=[bk.b], writes=[XBb[c]])
                ps_free(bk)

    def store_y(tl):
        T, G, npp = tl.T, tl.G, tl.np
        with Phase() as ph:
            YT, YTb = ph.sb("YT", [128, G, D_MODEL], F32)
            for g in range(G):
                for q in range(4):
                    bk = ps_alloc()
                    fns = [mmf(bk.t[0:npp, j * 128:(j + 1) * 128], X[:, 4 * q + j, g * npp:(g + 1) * npp], IDENT, j == 0, j == 3)
                           for j in range(4)]
                    MM(fns, reads=[Xb[4 * q + j] for j in range(4)] + [CTb], writes=[bk.b])
                    if q % 2 == 0:
                        A(lambda h, bk=bk, g=g, q=q: h.activation(out=YT[0:npp, g, q * 512:(q + 1) * 512], in_=bk.t[0:npp, :], func=AF.Copy),
                          reads=[bk.b], writes=[YTb])
                    else:
                        V(lambda h, bk=bk, g=g, q=q: h.tensor_copy(out=YT[0:npp, g, q * 512:(q + 1) * 512], in_=bk.t[0:npp, :]),
                          reads=[bk.b], writes=[YTb])
                    ps_free(bk)
            if tl.kind == "p":
                Dq("sp", y_p[tl.idx * 512:(tl.idx + 1) * 512, :].rearrange("(g p) d -> p g d", p=128), YT[:], reads=[YTb])
            else:
                Dq("sp", y_s, YT[0:npp, 0, :], reads=[YTb])

    def layer_norm(tl, l, i, bm, bq):
        T = tl.T
        gi = l * 2 + i
        with Phase() as ph:
            MEAN, MEANb = ph.sb("MEAN", [128, T], F32)
            RSTD, RSTDb = ph.sb("RSTD", [128, T], F32)
            T1, T1b = ph.sb("T1", [128, 2, T], F32, nb=2)
            A(lambda h: h.activation(out=MEAN[:], in_=bm.t[:, 0:T], func=AF.Copy, scale=1.0 / D_MODEL), reads=[bm.b], writes=[MEANb])
            V(lambda h: h.tensor_tensor(out=RSTD[:], in0=MEAN[:], in1=MEAN[:], op=ALU.mult), reads=[MEANb], writes=[RSTDb])
            V(lambda h: h.scalar_tensor_tensor(out=RSTD[:], in0=bq.t[:, 0:T], scalar=1.0 / D_MODEL, in1=RSTD[:], op0=ALU.mult, op1=ALU.subtract),
              reads=[bq.b, RSTDb], writes=[RSTDb])
            A(lambda h: h.activation(out=RSTD[:], in_=RSTD[:], func=AF.Sqrt, bias=LN_EPS), reads=[RSTDb], writes=[RSTDb])
            V(lambda h: h.reciprocal(out=RSTD[:], in_=RSTD[:]), reads=[RSTDb], writes=[RSTDb])
            ps_free(bm)
            ps_free(bq)
            for c in range(16):
                V(lambda h, c=c: h.tensor_tensor(out=T1[:, c % 2, :], in0=X[:, c, 0:T], in1=MEAN[:], op=ALU.subtract),
                  reads=[Xb[c], MEANb], writes=[T1b[c % 2]])
                V(lambda h, c=c: h.tensor_tensor(out=T1[:, c % 2, :], in0=T1[:, c % 2, :], in1=RSTD[:], op=ALU.mult),
                  reads=[T1b[c % 2], RSTDb], writes=[T1b[c % 2]])
                A(lambda h, c=c: h.activation(out=X[:, c, 0:T], in_=T1[:, c % 2, :], func=AF.Identity,
                                              scale=PVL[:, c, gi:gi + 1], bias=PVL[:, c, 8 + gi:9 + gi]),
                  reads=[T1b[c % 2], PVLb], writes=[Xb[c]])
                A(lambda h, c=c: h.activation(out=XB[:, c, 0:T], in_=X[:, c, 0:T], func=AF.Copy), reads=[Xb[c]], writes=[XBb[c]])

    def proj_res(tl, wk, W2, rhs):
        T = tl.T
        bm = ps_alloc()
        bq = ps_alloc()
        with Phase() as ph:
            SQ, SQb = ph.sb("SQl", [128, 2, T], BF16, nb=2)

            def evac(bi, banks):
                for j, bk in enumerate(banks):
                    c = 4 * bi + j
                    V(lambda h, bk=bk, c=c: h.scalar_tensor_tensor(out=X[:, c, 0:T], in0=X[:, c, 0:T], scalar=ALPHA, in1=bk.t[:, 0:T],
                                                                   op0=ALU.mult, op1=ALU.add),
                      reads=[Xb[c], bk.b], writes=[Xb[c]])
                    A(lambda h, c=c: h.activation(out=XB[:, c, 0:T], in_=X[:, c, 0:T], func=AF.Copy), reads=[Xb[c]], writes=[XBb[c]])
                    A(lambda h, c=c: h.activation(out=SQ[:, c % 2, :], in_=X[:, c, 0:T], func=AF.Square), reads=[Xb[c]], writes=[SQb[c % 2]])
                    MM([mmf(bm.t[:, 0:T], ONESB[:, :], XB[:, c, 0:T], c == 0, c == 15)], reads=[XBb[c], ONESBb], writes=[bm.b])
                    MM([mmf(bq.t[:, 0:T], ONESB[:, :], SQ[:, c % 2, :], c == 0, c == 15)], reads=[SQb[c % 2], ONESBb], writes=[bq.b])

            linear_fm(wk, W2, [[(n * 512, 512)] for n in range(4)], rhs, T, evac)
        return bm, bq

    def even_mixer(tl, e, MIX, MIXb):
        T, G, npp = tl.T, tl.G, tl.np
        Wi = W["ev_w_in"][e]
        isP = tl.kind == "p"
        cdec = cdec_p if isP else cdec_s
        with Phase() as ph:
            RK, RKb = ph.sb("RK", [128, 2, T], F32)
            Dq("sp", RK[:], rot[0:2, :, tl.rc0:tl.rc0 + T].rearrange("a p t -> p a t"), writes=[RKb])
            RQ, RQb = ph.sb("RQ", [128, 2, 2, T], F32, nb=2)
            TA, TAb = ph.sb("TA", [128, 2, T], F32, nb=2)
            QR, QRb = ph.sb("QR", [128, 2, 2, T], BF16, nb=2)
            KR, KRb = ph.sb("KR", [128, 2, 2, T], BF16, nb=2)
            VV, VVb = ph.sb("VV", [128, 2, G, 256], BF16, nb=2)
            KZ, KZb = ph.sb("KZ", [128, 2, G, 256], BF16, nb=2)
            GS, GSb = ph.sb("GS", [128, 2, 2, T], BF16, nb=2)
            IT, ITb = ph.sb("IT", [128, 2, 128], BF16, nb=2)
            SBF, SBFb = ph.sb("SBF", [128, 2, 512], BF16, nb=2)
            SQ, SQb = ph.sb("SQr", [128, 2, T], F32, nb=2)
            SQH, SQHb = ph.sb("SQH", [128, 2, T], BF16, nb=2)
            RN, RNb = ph.sb("RN", [128, T], F32)
            if not isP:
                KZM, KZMb = ph.sb("KZM", [TS, NS, 256], BF16)
                SIN, SINb = ph.sb("SIN", [128, 2, 4, 512], F32, nb=2)
                SBS, SBSb = ph.sb("SBS", [128, 2, 4, 512], BF16, nb=2)
                DMASK = lambda hh: CT[0:TS, CT_DMS + hh * 64:CT_DMS + (hh + 1) * 64]
                ZETA = lambda hh: CT[0:TS, CT_ZS + hh:CT_ZS + hh + 1]
            else:
                DMASK = lambda hh: CT[:, CT_DMP + hh * 128:CT_DMP + (hh + 1) * 128]
                ZETA = lambda hh: CT[:, CT_ZP + hh:CT_ZP + hh + 1]

            for hd in range(4):
                p2 = hd % 2
                Dq("sp", RQ[:, p2, :, :], rot[2 + 2 * hd:4 + 2 * hd, :, tl.rc0:tl.rc0 + T].rearrange("a p t -> p a t"), writes=[RQb[p2]])

                def rotary(b1, b2, tab, tabb, dst, dstb, ti):
                    cs, sn = tab[:, 0, :], tab[:, 1, :]
                    V(lambda h: h.tensor_tensor(out=TA[:, 0, :], in0=b1.t[:, 0:T], in1=cs, op=ALU.mult), reads=[b1.b, tabb], writes=[TAb[0]])
                    V(lambda h: h.tensor_tensor(out=TA[:, 1, :], in0=b2.t[:, 0:T], in1=sn, op=ALU.mult), reads=[b2.b, tabb], writes=[TAb[1]])
                    V(lambda h: h.tensor_tensor(out=dst[:, 0, :], in0=TA[:, 0, :], in1=TA[:, 1, :], op=ALU.subtract), reads=[TAb[0], TAb[1]], writes=[dstb])
                    V(lambda h: h.tensor_tensor(out=TA[:, 0, :], in0=b2.t[:, 0:T], in1=cs, op=ALU.mult), reads=[b2.b, tabb], writes=[TAb[0]])
                    V(lambda h: h.tensor_tensor(out=TA[:, 1, :], in0=b1.t[:, 0:T], in1=sn, op=ALU.mult), reads=[b1.b, tabb], writes=[TAb[1]])
                    V(lambda h: h.tensor_tensor(out=dst[:, 1, :], in0=TA[:, 0, :], in1=TA[:, 1, :], op=ALU.add), reads=[TAb[0], TAb[1]], writes=[dstb])

                def evacA(bi, banks, hd=hd, p2=p2):
                    rotary(banks[0], banks[1], RQ[:, p2], RQb[p2], QR[:, p2], QRb[p2], 0)
                    rotary(banks[2], banks[3], RK, RKb, KR[:, p2], KRb[p2], 1)

                linear_fm(f"evin{e}", Wi, [[(hd * 256, 256), (1024 + hd * 256, 256)]], xb_rhs(T), T, evacA)
                slot, sbuf = wload(f"evin{e}", [(Wi, 2048 + hd * 256, 256), (Wi, 3072 + hd * 256, 256)], 0, 16)
                for g in range(G):
                    bk = ps_alloc()
                    MM([mmf(bk.t[0:npp, 0:256], XB[:, kc, g * npp:(g + 1) * npp], slot[:, kc, 0:256], kc == 0, kc == 15) for kc in range(16)],
                       reads=[sbuf] + XBb, writes=[bk.b])
                    A(lambda h, bk=bk, g=g: h.activation(out=VV[0:npp, p2, g, :], in_=bk.t[0:npp, 0:256], func=AF.Copy), reads=[bk.b], writes=[VVb[p2]])
                    ps_free(bk)
                for j in range(2):
                    bk = ps_alloc()
                    MM([mmf(bk.t[:, 0:T], slot[:, kc, 256 + j * 128:256 + (j + 1) * 128], XB[:, kc, 0:T], kc == 0, kc == 15) for kc in range(16)],
                       reads=[sbuf] + XBb, writes=[bk.b])
                    A(lambda h, bk=bk, j=j: h.activation(out=GS[:, p2, j, :], in_=bk.t[:, 0:T], func=AF.Silu), reads=[bk.b], writes=[GSb[p2]])
                    ps_free(bk)
                for g in range(G):
                    bk = ps_alloc()
                    MM([mmf(bk.t[0:npp, kc2 * 128:(kc2 + 1) * 128], KR[:, p2, kc2, g * npp:(g + 1) * npp], IDB[:, :], kc2 == 0, kc2 == 1)
                        for kc2 in range(2)], reads=[KRb[p2], IDBb], writes=[bk.b])
                    A(lambda h, bk=bk, g=g: h.activation(out=KZ[0:npp, p2, g, :], in_=bk.t[0:npp, 0:256], func=AF.Identity, scale=ZETA(hd)),
                      reads=[bk.b, CTb], writes=[KZb[p2]])
                    ps_free(bk)
                PO = [ps_alloc(), ps_alloc()]
                Sst = RS[e][:, hd, :]
                Sstb = RSb[e][hd]
                if isP:
                    A(lambda h: h.activation(out=SBF[:, 0, :], in_=Sst, func=AF.Copy), reads=[Sstb], writes=[SBFb[0]])
                    for g in range(G):
                        gc = slice(g * 128, (g + 1) * 128)
                        sp_ = g % 2
                        bi_ = ps_alloc()
                        MM([mmf(bi_.t[:, 0:128], KR[:, p2, kc2, gc], QR[:, p2, kc2, gc], kc2 == 0, kc2 == 1) for kc2 in range(2)],
                           reads=[KRb[p2], QRb[p2]], writes=[bi_.b])
                        V(lambda h, bi_=bi_, g=g: h.tensor_tensor(out=IT[:, g % 2, :], in0=bi_.t[:, 0:128], in1=DMASK(hd), op=ALU.mult),
                          reads=[bi_.b, CTb], writes=[ITb[g % 2]])
                        ps_free(bi_)
                        for vc in range(2):
                            vcs = slice(vc * 128, (vc + 1) * 128)
                            MM([mmf(PO[vc].t[:, gc], VV[:, p2, g, vcs], IT[:, g % 2, :], g == 0, False),
                                mmf(PO[vc].t[:, gc], SBF[:, sp_, vc * 128:(vc + 1) * 128], QR[:, p2, 0, gc], False, False),
                                mmf(PO[vc].t[:, gc], SBF[:, sp_, 256 + vc * 128:256 + (vc + 1) * 128], QR[:, p2, 1, gc], False, g == G - 1)],
                               reads=[VVb[p2], ITb[g % 2], SBFb[sp_], QRb[p2]], writes=[PO[vc].b])
                        bs_ = ps_alloc()
                        MM([mmf(bs_.t[:, kc2 * 256:(kc2 + 1) * 256], KZ[:, p2, g, kc2 * 128:(kc2 + 1) * 128], VV[:, p2, g, :], kc2 == 0, kc2 == 1)
                            for kc2 in range(2)], reads=[KZb[p2], VVb[p2]], writes=[bs_.b])
                        V(lambda h, bs_=bs_: h.scalar_tensor_tensor(out=Sst, in0=Sst, scalar=cdec[hd], in1=bs_.t[:, :], op0=ALU.mult, op1=ALU.add),
                          reads=[Sstb, bs_.b], writes=[Sstb])
                        ps_free(bs_)
                        if g < G - 1:
                            A(lambda h, g=g: h.activation(out=SBF[:, (g + 1) % 2, :], in_=Sst, func=AF.Copy), reads=[Sstb], writes=[SBFb[(g + 1) % 2]])
                else:
                    bi_ = ps_alloc()
                    MM([mmf(bi_.t[0:TS, 0:TS], KR[:, p2, kc2, 0:TS], QR[:, p2, kc2, 0:TS], kc2 == 0, kc2 == 1) for kc2 in range(2)],
                       reads=[KRb[p2], QRb[p2]], writes=[bi_.b])
                    V(lambda h, bi_=bi_: h.tensor_tensor(out=IT[0:TS, 0, 0:TS], in0=bi_.t[0:TS, 0:TS], in1=DMASK(hd), op=ALU.mult),
                      reads=[bi_.b, CTb], writes=[ITb[0]])
                    ps_free(bi_)
                    V(lambda h: h.tensor_tensor(out=KZM[:, :, :],
                                                in0=KZ[0:TS, p2, 0, :].unsqueeze(1).to_broadcast([TS, NS, 256]),
                                                in1=CT[0:TS, CT_SQM:CT_SQM + NS].unsqueeze(2).to_broadcast([TS, NS, 256]), op=ALU.mult),
                      reads=[KZb[p2], CTb], writes=[KZMb])
                    for vc in range(2):
                        vcs = slice(vc * 128, (vc + 1) * 128)
                        MM([mmf(PO[vc].t[:, 0:TS], VV[0:TS, p2, 0, vcs], IT[0:TS, 0, 0:TS], True, False)],
                           reads=[VVb[p2], ITb[0]], writes=[PO[vc].b])
                    def ld_state(q4):
                        s2 = q4 % 2
                        for bb in range(4):
                            src = s_ret[e, 4 * q4 + bb, hd].rearrange("(c p) v -> p c v", p=128)
                            Dq("sp", SIN[:, s2, bb].rearrange("p (c v) -> p c v", c=2), src, writes=[SINb[s2]], append=(bb > 0))

                    ld_state(0)
                    for q4 in range(4):
                        s2 = q4 % 2
                        if q4 + 1 < 4:
                            ld_state(q4 + 1)
                        A(lambda h, s2=s2: h.activation(out=SBS[:, s2], in_=SIN[:, s2], func=AF.Copy), reads=[SINb[s2]], writes=[SBSb[s2]])
                        for bb in range(4):
                            b = 4 * q4 + bb
                            tc_ = slice(4 * b, 4 * b + 4)
                            last = (q4 == 3 and bb == 3)
                            for vc in range(2):
                                MM([mmf(PO[vc].t[:, tc_], SBS[:, s2, bb, vc * 128:(vc + 1) * 128], QR[:, p2, 0, tc_], False, False),
                                    mmf(PO[vc].t[:, tc_], SBS[:, s2, bb, 256 + vc * 128:256 + (vc + 1) * 128], QR[:, p2, 1, tc_], False, last)],
                                   reads=[SBSb[s2], QRb[p2]], writes=[PO[vc].b])
                            bs_ = ps_alloc()
                            MM([mmf(bs_.t[:, kc2 * 256:(kc2 + 1) * 256], KZM[:, b, kc2 * 128:(kc2 + 1) * 128], VV[0:TS, p2, 0, :], kc2 == 0, kc2 == 1)
                                for kc2 in range(2)], reads=[KZMb, VVb[p2]], writes=[bs_.b])
                            V(lambda h, bs_=bs_, bb=bb, s2=s2: h.scalar_tensor_tensor(out=SIN[:, s2, bb, :], in0=SIN[:, s2, bb, :], scalar=cdec[hd],
                                                                                    in1=bs_.t[:, :], op0=ALU.mult, op1=ALU.add),
                              reads=[SINb[s2], bs_.b], writes=[SINb[s2]])
                            ps_free(bs_)
                        for bb in range(4):
                            dst = ret_s[e, 4 * q4 + bb, hd].rearrange("(c p) v -> p c v", p=128)
                            Dq("sp", dst, SIN[:, s2, bb].rearrange("p (c v) -> p c v", c=2), reads=[SINb[s2]])
                bn = ps_alloc()
                for vc in range(2):
                    A(lambda h, vc=vc: h.activation(out=SQH[:, vc, :], in_=PO[vc].t[:, 0:T], func=AF.Square), reads=[PO[vc].b], writes=[SQHb[vc]])
                    MM([mmf(bn.t[:, 0:T], ONESB[:, :], SQH[:, vc, :], vc == 0, vc == 1)], reads=[SQHb[vc], ONESBb], writes=[bn.b])
                A(lambda h: h.activation(out=RN[:], in_=bn.t[:, 0:T], func=AF.Sqrt, scale=1.0 / 256.0, bias=NORM_EPS), reads=[bn.b], writes=[RNb])
                V(lambda h: h.reciprocal(out=RN[:], in_=RN[:]), reads=[RNb], writes=[RNb])
                ps_free(bn)
                for vc in range(2):
                    V(lambda h, vc=vc: h.tensor_tensor(out=SQ[:, vc, :], in0=PO[vc].t[:, 0:T], in1=RN[:], op=ALU.mult),
                      reads=[PO[vc].b, RNb], writes=[SQb[vc]])
                    V(lambda h, vc=vc: h.tensor_tensor(out=MIX[:, 2 * hd + vc, 0:T], in0=SQ[:, vc, :], in1=GS[:, p2, vc, :], op=ALU.mult),
                      reads=[SQb[vc], GSb[p2]], writes=[MIXb[2 * hd + vc]])
                ps_free(PO[0])
                ps_free(PO[1])

        with Phase() as ph:
            WA, WAb = ph.sb("WA", [128, 8, 128], BF16)
            WX, WXb = ph.sb("WX", [128, 8, 128], BF16)
            Dq("pool", WA[:], W["ev_rg_wa"][e].rearrange("n i j -> i n j"), writes=[WAb])
            Dq("pool", WX[:], W["ev_rg_wx"][e].rearrange("n i j -> i n j"), writes=[WXb])
            NB = 2
            if isP:
                XRP, XRPb = ph.sb("XRP", [128, NB, T + 3], F32, nb=NB)
            else:
                XRP, XRPb = ph.sb("XRP", [128, NB, NS, 7], F32, nb=NB)
                SRC, SRCb = ph.sb("SRC", [128, 8, NS * 3], F32)
                H0, H0b = ph.sb("H0", [128, 8, NS], F32)
                HL, HLb = ph.sb("HL", [128, 8, NS], F32)
                OT, OTb = ph.sb("OT", [128, 8, NS * 3], F32)
                TMPS, TMPSb = ph.sb("TMPS", [128, NS], F32)
                with Phase() as ph2:
                    r1, r1b = ph2.sb("r1", [NS * 3, 1024], F32)
                    r2, r2b = ph2.sb("r2", [NS, 1024], F32)
                    Dq("sp", r1[:], s_rgc[e], writes=[r1b])
                    Dq("sp", r2[:], s_rgh[e], writes=[r2b])
                    for c in range(8):
                        bk = ps_alloc()
                        MM([mmf(bk.t[:, 0:48], r1[0:48, c * 128:(c + 1) * 128], IDENT[0:48, 0:48], True, False),
                            mmf(bk.t[:, 64:80], r2[0:16, c * 128:(c + 1) * 128], IDENT[0:16, 0:16], False, True)],
                           reads=[r1b, r2b, CTb], writes=[bk.b])
                        A(lambda h, bk=bk, c=c: h.activation(out=SRC[:, c, :], in_=bk.t[:, 0:48], func=AF.Copy), reads=[bk.b], writes=[SRCb])
                        A(lambda h, bk=bk, c=c: h.activation(out=H0[:, c, :], in_=bk.t[:, 64:80], func=AF.Copy), reads=[bk.b], writes=[H0b])
                        ps_free(bk)
            GG, GGb = ph.sb("GG", [128, NB, T], F32, nb=NB)
            XC, XCb_ = ph.sb("XC", [128, NB, T], F32, nb=NB)
            XCB, XCBb = ph.sb("XCB", [128, NB, T], BF16, nb=NB)
            RG, RGb = ph.sb("RG", [128, NB, T], F32, nb=NB)
            IG, IGb = ph.sb("IG", [128, NB, T], F32, nb=NB)
            AA, AAb = ph.sb("AA", [128, NB, T], F32, nb=NB)
            MU, MUb = ph.sb("MU", [128, NB, T], F32, nb=NB)
            HH, HHb = ph.sb("HH", [128, NB, T], F32, nb=NB)

            def v3(ap, j0, j1, w):
                return ap.rearrange("p (b j) -> p b j", j=w)[:, :, j0:j1]

            def rg_chunk(c, bx, bg):
                s = c % NB
                pw = lambda j: PVE[:, e, c, j:j + 1]
                A(lambda h: h.activation(out=GG[:, s, :], in_=bg.t[:, 0:T], func=AF.Gelu_apprx_tanh), reads=[bg.b], writes=[GGb[s]])
                if isP:
                    A(lambda h: h.activation(out=XRP[:, s, 3:3 + T], in_=bx.t[:, 0:T], func=AF.Copy), reads=[bx.b], writes=[XRPb[s]])
                    V(lambda h: h.tensor_copy(out=XRP[:, s, 0:3], in_=RGT[e][0][:, c, :]), reads=[RGT[e][1]], writes=[XRPb[s]])
                    V(lambda h: h.tensor_copy(out=RGT[e][0][:, c, :], in_=XRP[:, s, T:T + 3]), reads=[XRPb[s]], writes=[RGT[e][1]])
                    xin = lambda j: XRP[:, s, j:j + T]
                    xc = XC[:, s, :]
                else:
                    A(lambda h: h.activation(out=XRP[:, s, :, 3:7], in_=v3(bx.t[:, 0:T], 0, 4, 4), func=AF.Copy), reads=[bx.b], writes=[XRPb[s]])
                    V(lambda h: h.tensor_copy(out=XRP[:, s, :, 0:3], in_=v3(SRC[:, c, :], 0, 3, 3)), reads=[SRCb], writes=[XRPb[s]])
                    V(lambda h: h.tensor_copy(out=v3(OT[:, c, :], 0, 3, 3), in_=XRP[:, s, :, 4:7]), reads=[XRPb[s]], writes=[OTb])
                    xin = lambda j: XRP[:, s, :, j:j + 4]
                    xc = v3(XC[:, s, :], 0, 4, 4)
                A(lambda h: h.activation(out=xc, in_=xin(0), func=AF.Identity, scale=pw(0), bias=pw(4)), reads=[XRPb[s], PVEb], writes=[XCb_[s]])
                for j in (1, 2, 3):
                    V(lambda h, j=j: h.scalar_tensor_tensor(out=xc, in0=xin(j), scalar=pw(j), in1=xc, op0=ALU.mult, op1=ALU.add),
                      reads=[XRPb[s], PVEb, XCb_[s]], writes=[XCb_[s]])
                A(lambda h: h.activation(out=XCB[:, s, :], in_=XC[:, s, :], func=AF.Copy), reads=[XCb_[s]], writes=[XCBb[s]])
                br = ps_alloc()
                MM([mmf(br.t[:, 0:T], WA[:, c, :], XCB[:, s, :], True, True)], reads=[WAb, XCBb[s]], writes=[br.b])
                A(lambda h: h.activation(out=RG[:, s, :], in_=br.t[:, 0:T], func=AF.Sigmoid, bias=pw(5)), reads=[br.b, PVEb], writes=[RGb[s]])
                ps_free(br)
                bi_ = ps_alloc()
                MM([mmf(bi_.t[:, 0:T], WX[:, c, :], XCB[:, s, :], True, True)], reads=[WXb, XCBb[s]], writes=[bi_.b])
                A(lambda h: h.activation(out=IG[:, s, :], in_=bi_.t[:, 0:T], func=AF.Sigmoid, bias=pw(6)), reads=[bi_.b, PVEb], writes=[IGb[s]])
                ps_free(bi_)
                A(lambda h: h.activation(out=AA[:, s, :], in_=RG[:, s, :], func=AF.Exp, scale=C8[:, e, 0, c:c + 1]), reads=[RGb[s], C8b], writes=[AAb[s]])
                A(lambda h: h.activation(out=MU[:, s, :], in_=RG[:, s, :], func=AF.Exp, scale=C8[:, e, 1, c:c + 1]), reads=[RGb[s], C8b], writes=[MUb[s]])
                V(lambda h: h.tensor_scalar(out=MU[:, s, :], in0=MU[:, s, :], scalar1=1.0, scalar2=None, op0=ALU.min), reads=[MUb[s]], writes=[MUb[s]])
                A(lambda h: h.activation(out=MU[:, s, :], in_=MU[:, s, :], func=AF.Sqrt, scale=-1.0, bias=1.0), reads=[MUb[s]], writes=[MUb[s]])
                if isP and tl.idx == 0:
                    V(lambda h: h.memset(MU[:, s, 0:1], 1.0), writes=[MUb[s]])
                V(lambda h: h.tensor_tensor(out=IG[:, s, :], in0=IG[:, s, :], in1=XC[:, s, :], op=ALU.mult), reads=[IGb[s], XCb_[s]], writes=[IGb[s]])
                V(lambda h: h.tensor_tensor(out=IG[:, s, :], in0=IG[:, s, :], in1=MU[:, s, :], op=ALU.mult), reads=[IGb[s], MUb[s]], writes=[IGb[s]])
                if isP:
                    V(lambda h: h.tensor_tensor_scan(out=HH[:, s, :], data0=AA[:, s, :], data1=IG[:, s, :], initial=RGH[e][0][:, c:c + 1],
                                                     op0=ALU.mult, op1=ALU.add),
                      reads=[AAb[s], IGb[s], RGH[e][1]], writes=[HHb[s]])
                    V(lambda h: h.tensor_copy(out=RGH[e][0][:, c:c + 1], in_=HH[:, s, T - 1:T]), reads=[HHb[s]], writes=[RGH[e][1]])
                else:
                    a3 = v3(AA[:, s, :], 0, 4, 4)
                    b3 = v3(IG[:, s, :], 0, 4, 4)
                    h3 = v3(HH[:, s, :], 0, 4, 4)
                    for j in range(4):
                        prev = H0[:, c, :] if j == 0 else h3[:, :, j - 1]
                        V(lambda h, j=j, prev=prev: h.tensor_tensor(out=TMPS[:, :], in0=a3[:, :, j], in1=prev, op=ALU.mult),
                          reads=[AAb[s], H0b, HHb[s]], writes=[TMPSb])
                        V(lambda h, j=j: h.tensor_tensor(out=h3[:, :, j], in0=TMPS[:, :], in1=b3[:, :, j], op=ALU.add),
                          reads=[TMPSb, IGb[s]], writes=[HHb[s]])
                    V(lambda h: h.tensor_copy(out=HL[:, c, :], in_=h3[:, :, 3]), reads=[HHb[s]], writes=[HLb])
                V(lambda h: h.tensor_tensor(out=MIX[:, 8 + c, 0:T], in0=HH[:, s, :], in1=GG[:, s, :], op=ALU.mult),
                  reads=[HHb[s], GGb[s]], writes=[MIXb[8 + c]])

            def evacR(bi, banks):
                for jj in range(2):
                    rg_chunk(2 * bi + jj, banks[jj], banks[2 + jj])

            linear_fm(f"evin{e}", Wi, [[(4096 + 256 * i, 256), (5120 + 256 * i, 256)] for i in range(4)], xb_rhs(T), T, evacR)

            if not isP:
                with Phase() as ph2:
                    o1, o1b = ph2.sb("o1", [NS * 3, 1024], F32)
                    o2, o2b = ph2.sb("o2", [NS, 1024], F32)
                    for q in range(2):
                        bk = ps_alloc()
                        MM([mmf(bk.t[0:48, j * 128:(j + 1) * 128], OT[:, 4 * q + j, :], IDENT, j == 0, j == 3) for j in range(4)],
                           reads=[OTb, CTb], writes=[bk.b])
                        A(lambda h, bk=bk, q=q: h.activation(out=o1[:, q * 512:(q + 1) * 512], in_=bk.t[0:48, :], func=AF.Copy), reads=[bk.b], writes=[o1b])
                        ps_free(bk)
                        bk = ps_alloc()
                        MM([mmf(bk.t[0:16, j * 128:(j + 1) * 128], HL[:, 4 * q + j, :], IDENT, j == 0, j == 3) for j in range(4)],
                           reads=[HLb, CTb], writes=[bk.b])
                        A(lambda h, bk=bk, q=q: h.activation(out=o2[:, q * 512:(q + 1) * 512], in_=bk.t[0:16, :], func=AF.Copy), reads=[bk.b], writes=[o2b])
                        ps_free(bk)
                    Dq("sp", rgc_s[e], o1[:], reads=[o1b])
                    Dq("sp", rgh_s[e], o2[:], reads=[o2b])

    def odd_mixer(tl, o, MIX, MIXb):
        T, G, npp, C = tl.T, tl.G, tl.np, tl.C
        NCH = T // C
        mid = C // 2 - 1
        Wi = W["od_w_in"][o]
        isP = tl.kind == "p"
        RM = RMP if isP else RMS
        RMb = RMPb if isP else RMSb
        with Phase() as ph:
            NB = 2
            QS, QSb = ph.sb("QS", [128, NB, T], F32, nb=NB)
            SG, SGb = ph.sb("SG", [128, NB, T], F32, nb=NB)
            KK, KKb = ph.sb("KK", [128, NB, T], F32, nb=NB)
            BC, BCb = ph.sb("BC", [128, NB, T], F32, nb=NB)
            D1, D1b = ph.sb("D1", [128, NB, T], F32, nb=NB)
            E1, E1b = ph.sb("E1", [128, NB, T], F32, nb=NB)
            QH, QHb = ph.sb("QH", [128, NB, T], BF16, nb=NB)
            KH, KHb = ph.sb("KH", [128, NB, T], BF16, nb=NB)
            QC, QCb = ph.sb("QC", [128, NB, T], F32 if isP else BF16, nb=NB)
            KT, KTb = ph.sb("KT", [128, NB, T], BF16, nb=NB)
            EBL, EBLb = ph.sb("EBL", [128, NB, NCH], F32, nb=NB)
            SGG, SGGb = ph.sb("SGG", [128, NB, T], BF16, nb=NB)
            VT, VTb = ph.sb("VT", [128, G, 256], BF16)
            ATM, ATMb = ph.sb("ATM", [128, 2, 128], BF16, nb=2)
            HS2, HS2b = ph.sb("HS2", [128, NB, 128], F32, nb=NB)
            LF, LFb = SG, SGb
            SQ, SQb = D1, D1b
            RN, RNb = E1, E1b
            if isP:
                KTM, KTMb = ph.sb("KTM", [128, NB, 2, 4, 128], BF16, nb=NB * 2)
            else:
                KTM, KTMb = ph.sb("KTM", [TS, NB, NS, 128], BF16, nb=NB)
                SIN, SINb = ph.sb("SINh", [128, NB, NS, 128], F32, nb=NB)
                SOUT, SOUTb = ph.sb("SOUT", [128, NB, NS, 128], F32, nb=NB)
                SB16, SB16b = ph.sb("SB16", [128, NB, NS, 128], BF16, nb=NB)

            for hp in range(8):
                hds = (2 * hp, 2 * hp + 1)
                if not isP:
                    for hh in range(2):
                        Dq("sp", SIN[:, hh], s_hg[o, :, hds[hh]].rearrange("b k v -> k b v"), writes=[SINb[hh]])

                def evacA(bi, banks, hds=hds):
                    for hh in range(2):
                        hd = hds[hh]
                        bq, bf = banks[hh], banks[2 + hh]
                        lb = LBT[:, o, 0, hd:hd + 1]
                        oml = LBT[:, o, 1, hd:hd + 1]
                        noml = LBT[:, o, 2, hd:hd + 1]
                        A(lambda h, hh=hh, bq=bq: h.activation(out=QS[:, hh, :], in_=bq.t[:, 0:T], func=AF.Silu), reads=[bq.b], writes=[QSb[hh]])
                        A(lambda h, hh=hh, bf=bf: h.activation(out=SG[:, hh, :], in_=bf.t[:, 0:T], func=AF.Sigmoid), reads=[bf.b], writes=[SGb[hh]])
                        V(lambda h, hh=hh, oml=oml, noml=noml: h.tensor_scalar(out=KK[:, hh, :], in0=SG[:, hh, :], scalar1=noml, scalar2=oml,
                                                                                op0=ALU.mult, op1=ALU.add),
                          reads=[SGb[hh], LBTb], writes=[KKb[hh]])
                        A(lambda h, hh=hh, lb=lb, oml=oml: h.activation(out=LF[:, hh, :], in_=SG[:, hh, :], func=AF.Ln, scale=oml, bias=lb),
                          reads=[SGb[hh], LBTb], writes=[LFb[hh]])
                        V(lambda h, hh=hh: h.tensor_tensor_scan(out=BC[:, hh, :], data0=RM[:, 0:T], data1=LF[:, hh, :], initial=0.0,
                                                                op0=ALU.mult, op1=ALU.add),
                          reads=[RMb, LFb[hh]], writes=[BCb[hh]])
                        B3 = BC[:, hh, :].rearrange("p (n j) -> p n j", j=C)
                        d3 = D1[:, hh, :].rearrange("p (n j) -> p n j", j=C)
                        V(lambda h, B3=B3, d3=d3: h.tensor_tensor(out=d3, in0=B3, in1=B3[:, :, mid:mid + 1].to_broadcast([128, NCH, C]), op=ALU.subtract),
                          reads=[BCb[hh]], writes=[D1b[hh]])
                        A(lambda h, hh=hh: h.activation(out=E1[:, hh, :], in_=D1[:, hh, :], func=AF.Exp), reads=[D1b[hh]], writes=[E1b[hh]])
                        A(lambda h, hh=hh: h.activation(out=D1[:, hh, :], in_=D1[:, hh, :], func=AF.Exp, scale=-1.0), reads=[D1b[hh]], writes=[D1b[hh]])
                        V(lambda h, hh=hh: h.tensor_tensor(out=QH[:, hh, :], in0=QS[:, hh, :], in1=E1[:, hh, :], op=ALU.mult),
                          reads=[QSb[hh], E1b[hh]], writes=[QHb[hh]])
                        V(lambda h, hh=hh: h.tensor_tensor(out=KH[:, hh, :], in0=KK[:, hh, :], in1=D1[:, hh, :], op=ALU.mult),
                          reads=[KKb[hh], D1b[hh]], writes=[KHb[hh]])
                        A(lambda h, hh=hh: h.activation(out=E1[:, hh, :], in_=BC[:, hh, :], func=AF.Exp), reads=[BCb[hh]], writes=[E1b[hh]])
                        V(lambda h, B3=B3, d3=d3, hh=hh: h.tensor_tensor(out=d3, in0=B3[:, :, C - 1:C].to_broadcast([128, NCH, C]), in1=B3, op=ALU.subtract),
                          reads=[BCb[hh]], writes=[D1b[hh]])
                        A(lambda h, hh=hh: h.activation(out=D1[:, hh, :], in_=D1[:, hh, :], func=AF.Exp), reads=[D1b[hh]], writes=[D1b[hh]])
                        V(lambda h, hh=hh: h.tensor_tensor(out=QC[:, hh, :], in0=QS[:, hh, :], in1=E1[:, hh, :], op=ALU.mult),
                          reads=[QSb[hh], E1b[hh]], writes=[QCb[hh]])
                        V(lambda h, hh=hh: h.tensor_tensor(out=KT[:, hh, :], in0=KK[:, hh, :], in1=D1[:, hh, :], op=ALU.mult),
                          reads=[KKb[hh], D1b[hh]], writes=[KTb[hh]])
                        A(lambda h, hh=hh, B3=B3: h.activation(out=EBL[:, hh, :], in_=B3[:, :, C - 1], func=AF.Exp), reads=[BCb[hh]], writes=[EBLb[hh]])

                linear_fm(f"odin{o}", Wi, [[(hp * 256, 256), (2048 + hp * 256, 256)]], xb_rhs(T), T, evacA)
                slot, sbuf = wload(f"odin{o}", [(Wi, 4096 + hp * 256, 256), (Wi, 6144 + hp * 256, 256)], 0, 16)
                for g in range(G):
                    bk = ps_alloc()
                    MM([mmf(bk.t[0:npp, 0:256], XB[:, kc, g * npp:(g + 1) * npp], slot[:, kc, 0:256], kc == 0, kc == 15) for kc in range(16)],
                       reads=[sbuf] + XBb, writes=[bk.b])
                    A(lambda h, bk=bk, g=g: h.activation(out=VT[0:npp, g, :], in_=bk.t[0:npp, 0:256], func=AF.Copy), reads=[bk.b], writes=[VTb])
                    ps_free(bk)
                for hh in range(2):
                    bk = ps_alloc()
                    MM([mmf(bk.t[:, 0:T], slot[:, kc, 256 + hh * 128:256 + (hh + 1) * 128], XB[:, kc, 0:T], kc == 0, kc == 15) for kc in range(16)],
                       reads=[sbuf] + XBb, writes=[bk.b])
                    A(lambda h, bk=bk, hh=hh: h.activation(out=SGG[:, hh, :], in_=bk.t[:, 0:T], func=AF.Sigmoid), reads=[bk.b], writes=[SGGb[hh]])
                    ps_free(bk)
                def ktm_build(hh, g):
                    bk = ps_alloc()
                    MM([mmf(bk.t[0:npp, 0:128], KT[:, hh, g * npp:(g + 1) * npp], IDB[:, :], True, True)], reads=[KTb[hh], IDBb], writes=[bk.b])
                    if isP:
                        V(lambda h: h.tensor_tensor(
                            out=KTM[:, hh, g % 2, :, :], in0=bk.t[:, 0:128].unsqueeze(1).to_broadcast([128, 4, 128]),
                            in1=CT[:, CT_CM:CT_CM + 4].unsqueeze(2).to_broadcast([128, 4, 128]), op=ALU.mult),
                          reads=[bk.b, CTb], writes=[KTMb[hh * 2 + g % 2]])
                    else:
                        V(lambda h: h.tensor_tensor(
                            out=KTM[:, hh, :, :], in0=bk.t[0:TS, 0:128].unsqueeze(1).to_broadcast([TS, NS, 128]),
                            in1=CT[0:TS, CT_SQM:CT_SQM + NS].unsqueeze(2).to_broadcast([TS, NS, 128]), op=ALU.mult),
                          reads=[bk.b, CTb], writes=[KTMb[hh]])
                    ps_free(bk)
                if not isP:
                    for hh in range(2):
                        ktm_build(hh, 0)
                PO = [ps_alloc(), ps_alloc()]
                if isP:
                    def st_ap(hh, par):
                        return HS[o][:, hds[hh], :] if par == 0 else HS2[:, hh, :]

                    def st_buf(hh, par):
                        return HSb[o][hds[hh]] if par == 0 else HS2b[hh]

                    for hh in range(2):
                        ktm_build(hh, 0)
                    for g in range(G):
                        gc = slice(g * 128, (g + 1) * 128)
                        for hh in range(2):
                            ba = ps_alloc()
                            MM([mmf(ba.t[:, 0:128], KH[:, hh, gc], QH[:, hh, gc], True, True)], reads=[KHb[hh], QHb[hh]], writes=[ba.b])
                            V(lambda h, ba=ba, hh=hh: h.tensor_tensor(out=ATM[:, hh, :], in0=ba.t[:, 0:128], in1=CT[:, CT_BCM:CT_BCM + 128], op=ALU.mult),
                              reads=[ba.b, CTb], writes=[ATMb[hh]])
                            ps_free(ba)
                            MM([mmf(PO[hh].t[:, gc], VT[:, g, hh * 128:(hh + 1) * 128], ATM[:, hh, :], g == 0, False)],
                               reads=[VTb, ATMb[hh]], writes=[PO[hh].b])
                        if g + 1 < G:
                            for hh in range(2):
                                ktm_build(hh, g + 1)
                        for j in range(4):
                            n = 4 * g + j
                            ncs = slice(n * 32, (n + 1) * 32)
                            for hh in range(2):
                                hd = hds[hh]
                                par = n % 2
                                MM([mmf(PO[hh].t[:, ncs], st_ap(hh, par), QC[:, hh, ncs], False, n == NCH - 1)],
                                   reads=[st_buf(hh, par), QCb[hh]], writes=[PO[hh].b])
                                bs_ = ps_alloc()
                                MM([mmf(bs_.t[:, 0:128], KTM[:, hh, g % 2, j, :], VT[:, g, hh * 128:(hh + 1) * 128], True, True)],
                                   reads=[KTMb[hh * 2 + g % 2], VTb], writes=[bs_.b])
                                V(lambda h, bs_=bs_, hh=hh, n=n, par=par: h.scalar_tensor_tensor(
                                    out=st_ap(hh, 1 - par), in0=st_ap(hh, par), scalar=EBL[:, hh, n:n + 1], in1=bs_.t[:, 0:128], op0=ALU.mult, op1=ALU.add),
                                  reads=[st_buf(hh, par), EBLb[hh], bs_.b], writes=[st_buf(hh, 1 - par)])
                                ps_free(bs_)
                else:
                    for hh in range(2):
                        hd = hds[hh]
                        A(lambda h, hh=hh: h.activation(out=SB16[:, hh], in_=SIN[:, hh], func=AF.Copy), reads=[SINb[hh]], writes=[SB16b[hh]])
                        ba = ps_alloc()
                        MM([mmf(ba.t[0:TS, 0:TS], KH[:, hh, 0:TS], QH[:, hh, 0:TS], True, True)], reads=[KHb[hh], QHb[hh]], writes=[ba.b])
                        V(lambda h, ba=ba, hh=hh: h.tensor_tensor(out=ATM[0:TS, hh, 0:TS], in0=ba.t[0:TS, 0:TS], in1=CT[0:TS, CT_BCS:CT_BCS + TS], op=ALU.mult),
                          reads=[ba.b, CTb], writes=[ATMb[hh]])
                        ps_free(ba)
                        fns = [mmf(PO[hh].t[:, 0:TS], VT[0:TS, 0, hh * 128:(hh + 1) * 128], ATM[0:TS, hh, 0:TS], True, False)]
                        for b in range(NS):
                            fns.append(mmf(PO[hh].t[:, 4 * b:4 * b + 4], SB16[:, hh, b, :], QC[:, hh, 4 * b:4 * b + 4], False, b == NS - 1))
                        MM(fns, reads=[VTb, ATMb[hh], SB16b[hh], QCb[hh]], writes=[PO[hh].b])
                        for q4 in range(4):
                            bs_ = ps_alloc()
                            MM([mmf(bs_.t[:, bb * 128:(bb + 1) * 128], KTM[:, hh, 4 * q4 + bb, :], VT[0:TS, 0, hh * 128:(hh + 1) * 128], bb == 0, bb == 3)
                                for bb in range(4)], reads=[KTMb[hh], VTb], writes=[bs_.b])
                            V(lambda h, hh=hh, q4=q4: h.tensor_tensor(
                                out=SOUT[:, hh, 4 * q4:4 * q4 + 4, :], in0=SIN[:, hh, 4 * q4:4 * q4 + 4, :],
                                in1=EBL[:, hh, 4 * q4:4 * q4 + 4].unsqueeze(2).to_broadcast([128, 4, 128]), op=ALU.mult),
                              reads=[SINb[hh], EBLb[hh]], writes=[SOUTb[hh]])
                            V(lambda h, hh=hh, q4=q4, bs_=bs_: h.tensor_tensor(
                                out=SOUT[:, hh, 4 * q4:4 * q4 + 4, :], in0=SOUT[:, hh, 4 * q4:4 * q4 + 4, :],
                                in1=bs_.t[:, :].rearrange("p (b v) -> p b v", v=128), op=ALU.add),
                              reads=[SOUTb[hh], bs_.b], writes=[SOUTb[hh]])
                            ps_free(bs_)
                        Dq("sp", hg_s[o, :, hd].rearrange("b k v -> k b v"), SOUT[:, hh], reads=[SOUTb[hh]])
                for hh in range(2):
                    hd = hds[hh]
                    A(lambda h, hh=hh: h.activation(out=SQ[:, hh, :], in_=PO[hh].t[:, 0:T], func=AF.Square), reads=[PO[hh].b], writes=[SQb[hh]])
                    bn = ps_alloc()
                    MM([mmf(bn.t[:, 0:T], ONES, SQ[:, hh, :], True, True)], reads=[SQb[hh], CTb], writes=[bn.b])
                    A(lambda h, hh=hh, bn=bn: h.activation(out=RN[:, hh, :], in_=bn.t[:, 0:T], func=AF.Sqrt, scale=1.0 / 128.0, bias=NORM_EPS),
                      reads=[bn.b], writes=[RNb[hh]])
                    ps_free(bn)
                    V(lambda h, hh=hh: h.reciprocal(out=RN[:, hh, :], in_=RN[:, hh, :]), reads=[RNb[hh]], writes=[RNb[hh]])
                    V(lambda h, hh=hh: h.tensor_tensor(out=SQ[:, hh, :], in0=PO[hh].t[:, 0:T], in1=RN[:, hh, :], op=ALU.mult),
                      reads=[PO[hh].b, RNb[hh]], writes=[SQb[hh]])
                    V(lambda h, hh=hh, hd=hd: h.scalar_tensor_tensor(out=MIX[:, hd, 0:T], in0=SQ[:, hh, :], scalar=PVO[:, 0, 2 + o:3 + o],
                                                                     in1=SGG[:, hh, :], op0=ALU.mult, op1=ALU.mult),
                      reads=[SQb[hh], PVOb, SGGb[hh]], writes=[MIXb[hd]])
                ps_free(PO[0])
                ps_free(PO[1])

    def ffn(tl, l):
        T, G, npp = tl.T, tl.G, tl.np
        isP = tl.kind == "p"
        Wu = W["ffn_w_up"][l]
        Wd = W["ffn_w_down"][l]
        with Phase() as ph:
            HID, HIDb = ph.sb("HID", [128, NFC, T], BF16, nb=NFC)
            NB = 3
            if isP:
                UP, UPb = ph.sb("UP", [128, NB, T + 2], F32, nb=NB)
            else:
                UP, UPb = ph.sb("UP", [128, NB, NS, 6], F32, nb=NB)
                SF, SFb = ph.sb("SF", [NS * 2, D_FF], F32)
                FO, FOb = ph.sb("FO", [NS * 2, D_FF], F32)
                Dq("sp", SF[:], s_ffc[l], writes=[SFb])
                UO, UOb = ph.sb("UO", [128, 2, NS * 2], F32, nb=2)
            UC, UCb = ph.sb("UC", [128, 2, T], F32, nb=2)
            GU, GUb = ph.sb("GU", [128, 2, T], BF16, nb=2)
            cnt = [0]

            def v3(ap, j0, j1, w):
                return ap.rearrange("p (b j) -> p b j", j=w)[:, :, j0:j1]

            def chunk(cc, bu, bv):
                s = cnt[0] % NB
                s2 = cnt[0] % 2
                cnt[0] += 1
                pw = lambda j: PVF[:, cc, 4 * l + j:4 * l + j + 1]
                if isP:
                    A(lambda h: h.activation(out=UP[:, s, 2:2 + T], in_=bu.t[:, 0:T], func=AF.Copy), reads=[bu.b], writes=[UPb[s]])
                    V(lambda h: h.tensor_copy(out=UP[:, s, 0:2], in_=FTL[l][0][:, cc, :]), reads=[FTL[l][1]], writes=[UPb[s]])
                    V(lambda h: h.tensor_copy(out=FTL[l][0][:, cc, :], in_=UP[:, s, T:T + 2]), reads=[UPb[s]], writes=[FTL[l][1]])
                    uin = lambda j: UP[:, s, j:j + T]
                    uc = UC[:, s2, :]
                else:
                    A(lambda h: h.activation(out=UP[:, s, :, 2:6], in_=v3(bu.t[:, 0:T], 0, 4, 4), func=AF.Copy), reads=[bu.b], writes=[UPb[s]])
                    bh = ps_alloc()
                    MM([mmf(bh.t[:, 0:32], SF[0:32, cc * 128:(cc + 1) * 128], IDENT[0:32, 0:32], True, True)], reads=[SFb, CTb], writes=[bh.b])
                    V(lambda h: h.tensor_copy(out=UP[:, s, :, 0:2], in_=v3(bh.t[:, 0:32], 0, 2, 2)), reads=[bh.b], writes=[UPb[s]])
                    ps_free(bh)
                    V(lambda h: h.tensor_copy(out=v3(UO[:, s2, :], 0, 2, 2), in_=UP[:, s, :, 4:6]), reads=[UPb[s]], writes=[UOb[s2]])
                    bo = ps_alloc()
                    MM([mmf(bo.t[0:32, 0:128], UO[:, s2, :], IDENT, True, True)], reads=[UOb[s2], CTb], writes=[bo.b])
                    A(lambda h: h.activation(out=FO[:, cc * 128:(cc + 1) * 128], in_=bo.t[0:32, 0:128], func=AF.Copy), reads=[bo.b], writes=[FOb])
                    ps_free(bo)
                    uin = lambda j: UP[:, s, :, j:j + 4]
                    uc = v3(UC[:, s2, :], 0, 4, 4)
                A(lambda h: h.activation(out=uc, in_=uin(0), func=AF.Identity, scale=pw(0), bias=pw(3)), reads=[UPb[s], PVFb], writes=[UCb[s2]])
                for j in (1, 2):
                    V(lambda h, j=j: h.scalar_tensor_tensor(out=uc, in0=uin(j), scalar=pw(j), in1=uc, op0=ALU.mult, op1=ALU.add),
                      reads=[UPb[s], PVFb, UCb[s2]], writes=[UCb[s2]])
                A(lambda h: h.activation(out=GU[:, s2, :], in_=UC[:, s2, :], func=AF.Gelu_apprx_tanh), reads=[UCb[s2]], writes=[GUb[s2]])
                V(lambda h: h.tensor_tensor(out=HID[:, cc, 0:T], in0=bv.t[:, 0:T], in1=GU[:, s2, :], op=ALU.mult),
                  reads=[bv.b, GUb[s2]], writes=[HIDb[cc]])

            def evacU(bi, banks):
                for jj in range(2):
                    chunk(2 * bi + jj, banks[jj], banks[2 + jj])

            linear_fm(f"up{l}", Wu, [[(256 * i, 256), (D_FF + 256 * i, 256)] for i in range(NFC // 2)], xb_rhs(T), T, evacU)
            if not isP:
                Dq("sp", ffc_s[l], FO[:], reads=[FOb])
            bm, bq = proj_res(tl, f"down{l}", Wd, [(HID[:, kc, 0:T], HIDb[kc]) for kc in range(NFC)])
        layer_norm(tl, l, 1, bm, bq)

    def final_outputs():
        for e in range(2):
            if 2 * e < cfg.nl:
                for hd in range(4):
                    Dq("sp", ret_p[e, hd].rearrange("(c p) v -> p c v", p=128), RS[e][:, hd, :].rearrange("p (c v) -> p c v", c=2), reads=[RSb[e][hd]])
                with Phase() as ph:
                    o1, o1b = ph.sb("fo1", [3, 1024], F32)
                    o2, o2b = ph.sb("fo2", [1, 1024], F32)
                    for q in range(2):
                        bk = ps_alloc()
                        MM([mmf(bk.t[0:3, j * 128:(j + 1) * 128], RGT[e][0][:, 4 * q + j, :], IDENT, j == 0, j == 3) for j in range(4)],
                           reads=[RGT[e][1], CTb], writes=[bk.b])
                        A(lambda h, bk=bk, q=q: h.activation(out=o1[:, q * 512:(q + 1) * 512], in_=bk.t[0:3, :], func=AF.Copy), reads=[bk.b], writes=[o1b])
                        ps_free(bk)
                        bk = ps_alloc()
                        MM([mmf(bk.t[0:1, j * 128:(j + 1) * 128], RGH[e][0][:, 4 * q + j:4 * q + j + 1], IDENT, j == 0, j == 3) for j in range(4)],
                           reads=[RGH[e][1], CTb], writes=[bk.b])
                        A(lambda h, bk=bk, q=q: h.activation(out=o2[:, q * 512:(q + 1) * 512], in_=bk.t[0:1, :], func=AF.Copy), reads=[bk.b], writes=[o2b])
                        ps_free(bk)
                    Dq("sp", rgc_p[e], o1[:], reads=[o1b])
                    Dq("sp", rgh_p[e:e + 1, :], o2[:], reads=[o2b])
            if 2 * e + 1 < cfg.nl:
                Dq("sp", hg_p[e].rearrange("h k v -> k h v"), HS[e][:], reads=HSb[e])
        for l in range(cfg.nl):
            with Phase() as ph:
                o3, o3b = ph.sb("fo3", [2, D_FF], F32)
                for q in range(NFC // 4):
                    bk = ps_alloc()
                    MM([mmf(bk.t[0:2, j * 128:(j + 1) * 128], FTL[l][0][:, 4 * q + j, :], IDENT, j == 0, j == 3) for j in range(4)],
                       reads=[FTL[l][1], CTb], writes=[bk.b])
                    A(lambda h, bk=bk, q=q: h.activation(out=o3[:, q * 512:(q + 1) * 512], in_=bk.t[0:2, :], func=AF.Copy), reads=[bk.b], writes=[o3b])
                    ps_free(bk)
                Dq("sp", ffc_p[l], o3[:], reads=[o3b])

    setup()
    for ti, tn in enumerate(cfg.tiles):
        cur_tile[0] = ti
        tl = Tile("p", int(tn[1:])) if tn[0] == "p" else Tile("s", 0)
        if cfg.on("load"):
            load_x(tl)
        for l in range(cfg.nl):
            with Phase() as phm:
                MIX, MIXb = phm.sb("MIX", [128, 16, tl.T], BF16, nb=16)
                if l % 2 == 0:
                    if cfg.on("mix"):
                        even_mixer(tl, l // 2, MIX, MIXb)
                    Wo = W["ev_w_out"][l // 2]
                else:
                    if cfg.on("mix"):
                        odd_mixer(tl, l // 2, MIX, MIXb)
                    Wo = W["od_w_out"][l // 2]
                bm, bq = proj_res(tl, f"out{l}", Wo, [(MIX[:, c, 0:tl.T], MIXb[c]) for c in range(16)])
            layer_norm(tl, l, 0, bm, bq)
            if cfg.on("ffn"):
                ffn(tl, l)
        if cfg.on("store"):
            store_y(tl)
    if cfg.on("final"):
        final_outputs()
    k.finish("sp")
    es.close()
    return nc, (k.n_inst, k.n_wait)


_CACHE = {}


def _get_program(cfg):
    key = cfg.key()
    if key not in _CACHE:
        rot, ct, cdec_p, cdec_s = _host_tables()
        nc, stats = build_program(cfg, cdec_p, cdec_s)
        _CACHE[key] = (nc, rot, ct, stats)
    return _CACHE[key]


def make_in_maps(inp, cores, rot, ct):
    f = lambda a: np.ascontiguousarray(np.asarray(a, dtype=np.float32))
    pv_even = np.stack([np.concatenate([inp["ev_rg_conv_w"][e], inp["ev_rg_conv_b"][e][None], inp["ev_rg_ba"][e][None],
                                        inp["ev_rg_bx"][e][None], inp["ev_rg_lambda"][e][None]], axis=0) for e in range(2)])
    pv_ln = np.concatenate([np.asarray(inp["ln_g"]).reshape(8, D_MODEL), np.asarray(inp["ln_b"]).reshape(8, D_MODEL)], axis=0)
    pv_ffn = np.concatenate([np.concatenate([inp["ffn_conv_w"][l], inp["ffn_conv_b"][l][None]], axis=0) for l in range(4)], axis=0)
    pv_od = np.concatenate([np.asarray(inp["od_lb_logits"]), np.tile(np.asarray(inp["od_norm_g"]), (1, 16))], axis=0)
    shared = {
        "ev_w_in": f(inp["ev_w_in"]), "ev_w_out": f(inp["ev_w_out"]), "ev_rg_wa": f(inp["ev_rg_wa"]), "ev_rg_wx": f(inp["ev_rg_wx"]),
        "od_w_in": f(inp["od_w_in"]), "od_w_out": f(inp["od_w_out"]), "ffn_w_up": f(inp["ffn_w_up"]), "ffn_w_down": f(inp["ffn_w_down"]),
        "pv_even": f(pv_even), "pv_ln": f(pv_ln), "pv_ffn": f(pv_ffn), "pv_od": f(pv_od), "rot": rot, "ctab": ct,
    }
    maps = []
    for c in cores:
        sl = slice(NS * c, NS * (c + 1))
        m = dict(shared)
        m["xp"] = f(inp["x_prompt"][c % 4])
        m["xs"] = f(np.asarray(inp["x_sample"][sl]).reshape(TS, D_MODEL))
        m["s_ret"] = f(np.asarray(inp["state_ret"])[:, sl])
        m["s_rgh"] = f(np.asarray(inp["state_rglru_h"])[:, sl])
        m["s_rgc"] = f(np.asarray(inp["state_rglru_conv"])[:, sl].reshape(2, NS * 3, 1024))
        m["s_hg"] = f(np.asarray(inp["state_hgrn"])[:, sl])
        m["s_ffc"] = f(np.asarray(inp["state_ffn_conv"])[:, sl].reshape(4, NS * 2, D_FF))
        maps.append(m)
    return maps


def assemble(results, cores):
    f32 = np.float32
    y_prompt = np.zeros((4, SEQ, D_MODEL), f32)
    y_sample = np.zeros((DEC_BATCH, DEC_SEQ, D_MODEL), f32)
    rp = np.zeros((2, 4, 4, 256, 256), f32)
    rs = np.zeros((2, DEC_BATCH, 4, 256, 256), f32)
    hp = np.zeros((2, 4, 1024), f32)
    hs = np.zeros((2, DEC_BATCH, 1024), f32)
    cp = np.zeros((2, 4, 3, 1024), f32)
    cs = np.zeros((2, DEC_BATCH, 3, 1024), f32)
    gp = np.zeros((2, 4, 16, 128, 128), f32)
    gs = np.zeros((2, DEC_BATCH, 16, 128, 128), f32)
    fp = np.zeros((4, 4, 2, D_FF), f32)
    fs = np.zeros((4, DEC_BATCH, 2, D_FF), f32)
    for c, r in zip(cores, results):
        sl = slice(NS * c, NS * (c + 1))
        if c < 4:
            y_prompt[c] = r["y_p"]
            rp[:, c] = r["ret_p"]
            hp[:, c] = r["rgh_p"]
            cp[:, c] = r["rgc_p"]
            gp[:, c] = r["hg_p"]
            fp[:, c] = r["ffc_p"]
        y_sample[sl] = r["y_s"].reshape(NS, DEC_SEQ, D_MODEL)
        rs[:, sl] = r["ret_s"]
        hs[:, sl] = r["rgh_s"]
        cs[:, sl] = r["rgc_s"].reshape(2, NS, 3, 1024)
        gs[:, sl] = r["hg_s"]
        fs[:, sl] = r["ffc_s"].reshape(4, NS, 2, D_FF)
    return (y_prompt, y_sample, rp, rs, hp, hs, cp, cs, gp, gs, fp, fs)


def kernel(**inputs):
    cfg = Cfg()
    nc, rot, ct, _ = _get_program(cfg)
    cores = list(range(8))
    maps = make_in_maps(inputs, cores, rot, ct)
    res = run_bass_kernel_spmd(nc, maps, core_ids=cores)
    return assemble(res.results, cores)
```

```python
import numpy as np
import concourse.bass as bass
import concourse.mybir as mybir
from concourse.bass_utils import run_bass_kernel_spmd
from contextlib import ExitStack

F32 = mybir.dt.float32
BF16 = mybir.dt.bfloat16
AF = mybir.ActivationFunctionType
ALU = mybir.AluOpType

D_MODEL = 2048
DEPTH = 4
SEQ = 2048
DEC_BATCH = 128
DEC_SEQ = 4
PAST_LEN = 16384
D_FF = 5632
NFC = D_FF // 128
ALPHA = float((2 * DEPTH) ** 0.25)
LN_EPS = 1e-5
NORM_EPS = 1e-6
NS = 16
TS = NS * DEC_SEQ

CT_ID = 0
CT_ONES = 128
CT_DMP = 256
CT_BCM = 768
CT_CM = 896
CT_ZP = 900
CT_DMS = 904
CT_BCS = 1160
CT_SQM = 1224
CT_ZS = 1240
NCT = 1244


def _host_tables():
    f32 = np.float32
    half = 128
    inv = np.power(f32(10000.0), -np.arange(half, dtype=f32) / f32(half)).astype(f32)
    pos = np.concatenate([np.arange(SEQ), np.tile(PAST_LEN + np.arange(DEC_SEQ), NS)]).astype(f32)
    ang = (pos[:, None] * inv[None, :]).astype(f32)
    cos = np.cos(ang).astype(f32).T
    sin = np.sin(ang).astype(f32).T
    lg = np.log(f32(1.0) - np.power(f32(2.0), -5.0 - np.arange(4, dtype=f32))).astype(f32)
    idx_p = (np.arange(SEQ) % 128).astype(f32)
    idx_s = np.tile(np.arange(DEC_SEQ), NS).astype(f32)
    idx = np.concatenate([idx_p, idx_s])
    rot = np.zeros((10, 128, SEQ + TS), f32)
    ksc = f32(256.0 ** -0.5)
    rot[0] = cos * ksc
    rot[1] = sin * ksc
    for h in range(4):
        xi = np.exp((idx + 1.0) * lg[h]).astype(f32)
        rot[2 + 2 * h] = cos * xi[None, :]
        rot[3 + 2 * h] = sin * xi[None, :]
    ct = np.zeros((128, NCT), f32)
    ct[:, CT_ID:CT_ID + 128] = np.eye(128, dtype=f32)
    ct[:, CT_ONES:CT_ONES + 128] = 1.0
    s = np.arange(128)
    t = np.arange(128)
    for h in range(4):
        m = np.where(t[None, :] >= s[:, None], np.exp(-(s[:, None] + 1.0) * lg[h]), 0.0)
        ct[:, CT_DMP + h * 128:CT_DMP + (h + 1) * 128] = m
        ct[:, CT_ZP + h] = np.exp((127.0 - s) * lg[h])
    ct[:, CT_BCM:CT_BCM + 128] = ((s[:, None] // 32 == t[None, :] // 32) & (t[None, :] >= s[:, None])).astype(f32)
    for j in range(4):
        ct[:, CT_CM + j] = (s // 32 == j).astype(f32)
    s = np.arange(64)
    t = np.arange(64)
    same = (s[:, None] // 4 == t[None, :] // 4) & (t[None, :] >= s[:, None])
    for h in range(4):
        m = np.where(same, np.exp(-((s[:, None] % 4) + 1.0) * lg[h]), 0.0)
        ct[:64, CT_DMS + h * 64:CT_DMS + (h + 1) * 64] = m
        ct[:64, CT_ZS + h] = np.exp((3.0 - (s % 4)) * lg[h])
    ct[:64, CT_BCS:CT_BCS + 64] = same.astype(f32)
    for b in range(16):
        ct[:64, CT_SQM + b] = (s // 4 == b).astype(f32)
    cdec_p = [float(np.exp(f32(128.0) * lg[h])) for h in range(4)]
    cdec_s = [float(np.exp(f32(4.0) * lg[h])) for h in range(4)]
    return rot, ct, cdec_p, cdec_s


class Buf:
    __slots__ = ("name", "lw", "rd", "excl")

    def __init__(self, name, ghost=(), excl=False):
        self.name = name
        self.lw = []
        self.rd = list(ghost)
        self.excl = excl


class Eng:
    def __init__(self, name, h, sem):
        self.name = name
        self.h = h
        self.sem = sem
        self.count = 0
        self.known = {}


class K:
    def __init__(self, nc, es, n_dma_sems=10):
        self.nc = nc
        self.eng = {}
        for name, h in (("pe", nc.tensor), ("act", nc.scalar), ("dve", nc.vector),
                        ("pool", nc.gpsimd), ("sp", nc.sync)):
            s = es.enter_context(nc.semaphore("s_" + name))
            self.eng[name] = Eng(name, h, s)
        self.dma_ring = {}
        for q in ("sp", "pool"):
            ring = []
            for i in range(n_dma_sems):
                s = es.enter_context(nc.semaphore(f"d_{q}{i}"))
                ring.append([s, 0])
            self.dma_ring[q] = [ring, 0]
        self.n_wait = 0
        self.n_inst = 0
        self.ghost = {}

    def _wait(self, e, ev):
        sem, val, snap = ev
        if e.known.get(id(sem), 0) >= val:
            return
        e.h.wait_ge(sem, val)
        self.n_wait += 1
        for kk, v in snap.items():
            if e.known.get(kk, 0) < v:
                e.known[kk] = v
        e.known[id(sem)] = val

    def _deps(self, e, reads, writes, append):
        for b in reads:
            for ev in b.lw:
                self._wait(e, ev)
            if b.excl:
                for ev in b.rd:
                    if ev[0] is not e.sem:
                        self._wait(e, ev)
        for b in writes:
            if not append:
                for ev in b.lw:
                    self._wait(e, ev)
            for ev in b.rd:
                self._wait(e, ev)

    def _commit(self, ev, reads, writes, append):
        for b in reads:
            b.rd.append(ev)
        for b in writes:
            if append:
                b.lw.append(ev)
            else:
                b.lw = [ev]
            b.rd = []

    def op(self, en, fn, reads=(), writes=()):
        e = self.eng[en]
        self._deps(e, reads, writes, False)
        ins = fn(e.h)
        e.count += 1
        ins.then_inc(e.sem, 1)
        self.n_inst += 1
        ev = (e.sem, e.count, dict(e.known))
        self._commit(ev, reads, writes, False)
        return ev

    def mm(self, fns, reads=(), writes=()):
        e = self.eng["pe"]
        self._deps(e, reads, writes, False)
        ins = None
        for fn in fns:
            ins = fn(e.h)
            self.n_inst += 1
        e.count += 1
        ins.then_inc(e.sem, 1)
        ev = (e.sem, e.count, dict(e.known))
        self._commit(ev, reads, writes, False)
        return ev

    def dma(self, q, out, in_, reads=(), writes=(), append=False):
        e = self.eng[q]
        ring, pos = self.dma_ring[q]
        slot = ring[pos % len(ring)]
        self.dma_ring[q][1] = pos + 1
        sem, val = slot
        if val > 0:
            self._wait(e, (sem, val, {}))
        self._deps(e, reads, writes, append)
        e.h.dma_start(out=out, in_=in_).then_inc(sem, 16)
        self.n_inst += 1
        slot[1] = val + 16
        ev = (sem, val + 16, dict(e.known))
        self._commit(ev, reads, writes, append)
        return ev

    def retire(self, bufs):
        for b in bufs:
            for ev in list(b.lw) + list(b.rd):
                sem, val, _ = ev
                cur = self.ghost.get(id(sem))
                if cur is None or cur[1] < val:
                    self.ghost[id(sem)] = (sem, val, {})

    def ghost_events(self):
        return list(self.ghost.values())

    def finish(self, en="sp"):
        e = self.eng[en]
        for q, (ring, pos) in self.dma_ring.items():
            for sem, val in ring:
                if val > 0:
                    self._wait(e, (sem, val, {}))
        for name, o in self.eng.items():
            if o.count > 0 and name != en:
                self._wait(e, (o.sem, o.count, {}))


class Tile:
    def __init__(self, kind, idx):
        self.kind = kind
        self.idx = idx
        if kind == "p":
            self.T, self.G, self.np, self.C = 512, 4, 128, 32
            self.rc0 = idx * 512
        else:
            self.T, self.G, self.np, self.C = TS, 1, TS, 4
            self.rc0 = SEQ


class Cfg:
    def __init__(self, nl=4, tiles=("p0", "p1", "p2", "p3", "s"), dbg=None):
        self.nl = nl
        self.tiles = tuple(tiles)
        self.dbg = dbg

    def on(self, name):
        return self.dbg is None or name in self.dbg

    def key(self):
        return (self.nl, self.tiles, None if self.dbg is None else tuple(sorted(self.dbg)))


def build_program(cfg, cdec_p, cdec_s):
    nc = bass.Bass("TRN2", target_bir_lowering=False)
    es = ExitStack()

    def din(name, shape):
        return nc.dram_tensor(name, list(shape), F32, kind="ExternalInput").ap()

    def dout(name, shape):
        return nc.dram_tensor(name, list(shape), F32, kind="ExternalOutput").ap()

    xp = din("xp", [SEQ, D_MODEL])
    xs = din("xs", [TS, D_MODEL])
    s_ret = din("s_ret", [2, NS, 4, 256, 256])
    s_rgh = din("s_rgh", [2, NS, 1024])
    s_rgc = din("s_rgc", [2, NS * 3, 1024])
    s_hg = din("s_hg", [2, NS, 16, 128, 128])
    s_ffc = din("s_ffc", [4, NS * 2, D_FF])
    W = {
        "ev_w_in": din("ev_w_in", [2, D_MODEL, 6144]),
        "ev_w_out": din("ev_w_out", [2, D_MODEL, D_MODEL]),
        "ev_rg_wa": din("ev_rg_wa", [2, 8, 128, 128]),
        "ev_rg_wx": din("ev_rg_wx", [2, 8, 128, 128]),
        "od_w_in": din("od_w_in", [2, D_MODEL, 8192]),
        "od_w_out": din("od_w_out", [2, D_MODEL, D_MODEL]),
        "ffn_w_up": din("ffn_w_up", [4, D_MODEL, 2 * D_FF]),
        "ffn_w_down": din("ffn_w_down", [4, D_FF, D_MODEL]),
    }
    pv_even = din("pv_even", [2, 8, 1024])
    pv_ln = din("pv_ln", [16, D_MODEL])
    pv_ffn = din("pv_ffn", [16, D_FF])
    pv_od = din("pv_od", [4, D_MODEL])
    rot = din("rot", [10, 128, SEQ + TS])
    ctab = din("ctab", [128, NCT])

    y_p = dout("y_p", [SEQ, D_MODEL])
    y_s = dout("y_s", [TS, D_MODEL])
    ret_p = dout("ret_p", [2, 4, 256, 256])
    ret_s = dout("ret_s", [2, NS, 4, 256, 256])
    rgh_p = dout("rgh_p", [2, 1024])
    rgh_s = dout("rgh_s", [2, NS, 1024])
    rgc_p = dout("rgc_p", [2, 3, 1024])
    rgc_s = dout("rgc_s", [2, NS * 3, 1024])
    hg_p = dout("hg_p", [2, 16, 128, 128])
    hg_s = dout("hg_s", [2, NS, 16, 128, 128])
    ffc_p = dout("ffc_p", [4, 2, D_FF])
    ffc_s = dout("ffc_s", [4, NS * 2, D_FF])

    k = K(nc, es)
    uid = [0]

    def sb_raw(stack, name, shape, dt):
        uid[0] += 1
        return stack.enter_context(nc.sbuf_tensor(f"{name}_{uid[0]}", list(shape), dt))

    class Phase:
        def __init__(self):
            self.stack = ExitStack()
            self.bufs = []

        def sb(self, name, shape, dt, nb=1):
            t = sb_raw(self.stack, name, shape, dt)
            g = k.ghost_events()
            bs = [Buf(name, g) for _ in range(nb)]
            self.bufs.extend(bs)
            return (t, bs[0]) if nb == 1 else (t, bs)

        def __enter__(self):
            return self

        def __exit__(self, *a):
            k.retire(self.bufs)
            self.stack.close()
            return False

    def A(fn, reads=(), writes=()):
        return k.op("act", fn, reads, writes)

    def V(fn, reads=(), writes=()):
        return k.op("dve", fn, reads, writes)

    def MM(fns, reads=(), writes=()):
        return k.mm(fns, reads, writes)

    def Dq(q, out, in_, reads=(), writes=(), append=False):
        return k.dma(q, out, in_, reads, writes, append)

    def mmf(out, lhsT, rhs, start, stop):
        return lambda h: h.matmul(out, lhsT=lhsT, rhs=rhs, start=start, stop=stop, skip_group_check=True)

    def psb(name, shape, dt):
        return sb_raw(es, name, shape, dt), Buf(name)

    X = sb_raw(es, "X", [128, 16, 512], F32)
    Xb = [Buf(f"X{c}") for c in range(16)]
    XB = sb_raw(es, "XB", [128, 16, 512], BF16)
    XBb = [Buf(f"XB{c}") for c in range(16)]
    NSLOT = 3
    WS = [sb_raw(es, f"WS{i}", [128, 16, 512], BF16) for i in range(NSLOT)]
    WSb = [Buf(f"WS{i}") for i in range(NSLOT)]
    wpos = [0]
    RS = [sb_raw(es, f"RS{e}", [128, 4, 512], F32) for e in range(2)]
    RSb = [[Buf(f"RS{e}_{h}") for h in range(4)] for e in range(2)]
    HS = [sb_raw(es, f"HS{o}", [128, 16, 128], F32) for o in range(2)]
    HSb = [[Buf(f"HS{o}_{h}") for h in range(16)] for o in range(2)]
    RGH = [psb(f"RGH{e}", [128, 8], F32) for e in range(2)]
    RGT = [psb(f"RGT{e}", [128, 8, 3], F32) for e in range(2)]
    FTL = [psb(f"FTL{l}", [128, NFC, 2], F32) for l in range(4)]
    CT, CTb = psb("CT", [128, NCT], F32)
    IDB, IDBb = psb("IDB", [128, 128], BF16)
    ONESB, ONESBb = psb("ONESB", [128, 128], BF16)
    PVE, PVEb = psb("PVE", [128, 2, 8, 8], F32)
    PVL, PVLb = psb("PVL", [128, 16, 16], F32)
    PVF, PVFb = psb("PVF", [128, NFC, 16], F32)
    PVO, PVOb = psb("PVO", [128, 16, 4], F32)
    LBT, LBTb = psb("LBT", [128, 2, 3, 16], F32)
    RMP, RMPb = psb("RMP", [128, 512], F32)
    RMS, RMSb = psb("RMS", [128, TS], F32)
    C8, C8b = psb("C8", [128, 2, 2, 8], F32)

    IDENT = CT[:, CT_ID:CT_ID + 128]
    ONES = CT[:, CT_ONES:CT_ONES + 128]

    PSB = [es.enter_context(nc.psum_tensor(f"ps{i}", [128, 512], F32)) for i in range(8)]
    PSb = [Buf(f"ps{i}", excl=True) for i in range(8)]
    ps_free_list = list(range(8))

    class Bank:
        def __init__(self, i):
            self.i = i
            self.t = PSB[i]
            self.b = PSb[i]

    def ps_alloc():
        i = ps_free_list.pop(0)
        return Bank(i)

    def ps_free(bk):
        ps_free_list.append(bk.i)

    WC_TOTAL = (D_MODEL * 8192 + D_MODEL * D_MODEL + D_MODEL * 2 * D_FF + D_FF * D_MODEL) // 128
    WCL = [nc.dram_tensor(f"wcache{l}", [128, WC_TOTAL], BF16).ap() for l in range(cfg.nl)]

    def wk_layer(wk):
        if wk.startswith("evin"):
            return 2 * int(wk[4:])
        if wk.startswith("odin"):
            return 2 * int(wk[4:]) + 1
        return int(wk[-1])

    wcache = {}
    wc_off = [0, 0, 0, 0]
    use_cache = len(cfg.tiles) > 1
    cur_tile = [0]

    def wload(wk, pieces, k0, kn):
        i = wpos[0] % NSLOT
        wpos[0] += 1
        ncols = sum(n for _, _, n in pieces)
        key = (wk, tuple((c0, n) for _, c0, n in pieces), k0)
        lyr = wk_layer(wk)
        WC = WCL[lyr]
        if use_cache and key in wcache:
            off0, cbuf = wcache[key]
            Dq("pool", WS[i][:, 0:kn, 0:ncols], WC[:, off0:off0 + kn * ncols].rearrange("p (c n) -> p c n", n=ncols),
               reads=[cbuf], writes=[WSb[i]])
            return WS[i], WSb[i]
        off = 0
        first = True
        for (W2, c0, n) in pieces:
            src = W2[k0 * 128:(k0 + kn) * 128, c0:c0 + n].rearrange("(c p) n -> p c n", p=128)
            Dq("pool", WS[i][:, 0:kn, off:off + n], src, writes=[WSb[i]], append=not first)
            first = False
            off += n
        if use_cache and ((cur_tile[0] == 0 and lyr < 2) or cur_tile[0] >= 1):
            cbuf = Buf("wc")
            off0 = wc_off[lyr]
            wc_off[lyr] += kn * ncols
            assert wc_off[lyr] <= WC_TOTAL
            Dq("sp", WC[:, off0:off0 + kn * ncols].rearrange("p (c n) -> p c n", n=ncols), WS[i][:, 0:kn, 0:ncols],
               reads=[WSb[i]], writes=[cbuf])
            wcache[key] = (off0, cbuf)
        return WS[i], WSb[i]

    def linear_fm(wk, W2, blocks, rhs, T, evac):
        KC = len(rhs)
        kgroups = [(k0, min(16, KC - k0)) for k0 in range(0, KC, 16)]
        for bi, blk in enumerate(blocks):
            nch = sum(n for _, n in blk) // 128
            banks = [ps_alloc() for _ in range(nch)]
            for (k0, kn) in kgroups:
                slot, sbuf = wload(wk, [(W2, c0, n) for c0, n in blk], k0, kn)
                for j in range(nch):
                    fns = [mmf(banks[j].t[:, 0:T], slot[:, kc, j * 128:(j + 1) * 128], rhs[k0 + kc][0],
                               (k0 + kc == 0), (k0 + kc == KC - 1)) for kc in range(kn)]
                    MM(fns, reads=[sbuf] + [rhs[k0 + kc][1] for kc in range(kn)], writes=[banks[j].b])
            evac(bi, banks)
            for bk in banks:
                ps_free(bk)

    xb_rhs = lambda T: [(XB[:, c, 0:T], XBb[c]) for c in range(16)]

    def setup():
        Dq("sp", CT[:], ctab, writes=[CTb])
        V(lambda h: h.tensor_copy(out=IDB[:], in_=IDENT), reads=[CTb], writes=[IDBb])
        V(lambda h: h.memset(ONESB[:], 1.0), writes=[ONESBb])
        for e in range(2):
            V(lambda h, e=e: h.memset(RS[e][:], 0.0), writes=RSb[e])
            V(lambda h, e=e: h.memset(HS[e][:], 0.0), writes=HSb[e])
            V(lambda h, e=e: h.memset(RGH[e][0][:], 0.0), writes=[RGH[e][1]])
            V(lambda h, e=e: h.memset(RGT[e][0][:], 0.0), writes=[RGT[e][1]])
        for l in range(4):
            V(lambda h, l=l: h.memset(FTL[l][0][:], 0.0), writes=[FTL[l][1]])
        V(lambda h: h.memset(RMP[:], 1.0), writes=[RMPb])
        V(lambda h: h.memset(RMP[:].rearrange("p (n j) -> p n j", j=32)[:, :, 0:1], 0.0), writes=[RMPb])
        V(lambda h: h.memset(RMS[:], 1.0), writes=[RMSb])
        V(lambda h: h.memset(RMS[:].rearrange("p (n j) -> p n j", j=4)[:, :, 0:1], 0.0), writes=[RMSb])

        def rows_to_fm(src2d, R, nchunks, dst_fn, dstb):
            with Phase() as ph:
                rows, rowsb = ph.sb("rows", [R, nchunks * 128], F32)
                Dq("sp", rows[:], src2d, writes=[rowsb])
                per = 512 // R
                c = 0
                while c < nchunks:
                    n = min(per, nchunks - c)
                    bk = ps_alloc()
                    fns = [mmf(bk.t[:, j * R:(j + 1) * R], rows[0:R, (c + j) * 128:(c + j + 1) * 128], IDENT[0:R, 0:R],
                               j == 0, j == n - 1) for j in range(n)]
                    MM(fns, reads=[rowsb, CTb], writes=[bk.b])
                    A(lambda h, bk=bk, c=c, n=n: h.activation(out=dst_fn(c, n), in_=bk.t[:, 0:n * R].rearrange("p (j r) -> p j r", r=R), func=AF.Copy),
                      reads=[bk.b], writes=[dstb])
                    ps_free(bk)
                    c += n

        for e in range(2):
            rows_to_fm(pv_even[e], 8, 8, lambda c, n, e=e: PVE[:, e, c:c + n, :], PVEb)
        rows_to_fm(pv_ln, 16, 16, lambda c, n: PVL[:, c:c + n, :], PVLb)
        rows_to_fm(pv_ffn, 16, NFC, lambda c, n: PVF[:, c:c + n, :], PVFb)
        rows_to_fm(pv_od, 4, 16, lambda c, n: PVO[:, c:c + n, :], PVOb)
        V(lambda h: h.memset(LBT[:, 0, 0, :], 0.0), writes=[LBTb])
        V(lambda h: h.memset(LBT[:, 0, 1, :], 1.0), writes=[LBTb])
        V(lambda h: h.memset(LBT[:, 0, 2, :], -1.0), writes=[LBTb])
        V(lambda h: h.tensor_tensor(out=LBT[:, 1, 0, :], in0=PVO[:, :, 1], in1=PVO[:, :, 0], op=ALU.subtract), reads=[PVOb], writes=[LBTb])
        A(lambda h: h.activation(out=LBT[:, 1, 0, :], in_=LBT[:, 1, 0, :], func=AF.Sigmoid), reads=[LBTb], writes=[LBTb])
        V(lambda h: h.tensor_scalar(out=LBT[:, 1, 1, :], in0=LBT[:, 1, 0, :], scalar1=-1.0, scalar2=1.0, op0=ALU.mult, op1=ALU.add), reads=[LBTb], writes=[LBTb])
        V(lambda h: h.tensor_scalar(out=LBT[:, 1, 2, :], in0=LBT[:, 1, 0, :], scalar1=1.0, scalar2=-1.0, op0=ALU.mult, op1=ALU.add), reads=[LBTb], writes=[LBTb])
        for e in range(2):
            A(lambda h, e=e: h.activation(out=C8[:, e, 0, :], in_=PVE[:, e, :, 7], func=AF.Exp, scale=-1.0), reads=[PVEb], writes=[C8b])
            A(lambda h, e=e: h.activation(out=C8[:, e, 0, :], in_=C8[:, e, 0, :], func=AF.Ln, bias=1.0), reads=[C8b], writes=[C8b])
            V(lambda h, e=e: h.tensor_scalar(out=C8[:, e, 1, :], in0=C8[:, e, 0, :], scalar1=-16.0, scalar2=None, op0=ALU.mult), reads=[C8b], writes=[C8b])
            V(lambda h, e=e: h.tensor_scalar(out=C8[:, e, 0, :], in0=C8[:, e, 0, :], scalar1=-8.0, scalar2=None, op0=ALU.mult), reads=[C8b], writes=[C8b])

    def load_x(tl):
        T, G, npp = tl.T, tl.G, tl.np
        with Phase() as ph:
            XT, XTb = ph.sb("XT", [128, G, D_MODEL], F32)
            if tl.kind == "p":
                Dq("sp", XT[:], xp[tl.idx * 512:(tl.idx + 1) * 512, :].rearrange("(g p) d -> p g d", p=128), writes=[XTb])
            else:
                Dq("sp", XT[0:npp, 0, :], xs, writes=[XTb])
            for c in range(16):
                bk = ps_alloc()
                fns = [mmf(bk.t[:, g * npp:(g + 1) * npp], XT[0:npp, g, c * 128:(c + 1) * 128], IDENT[0:npp, 0:npp], g == 0, g == G - 1)
                       for g in range(G)]
                MM(fns, reads=[XTb, CTb], writes=[bk.b])
                A(lambda h, bk=bk, c=c: h.activation(out=X[:, c, 0:T], in_=bk.t[:, 0:T], func=AF.Copy), reads=[bk.b], writes=[Xb[c]])
                V(lambda h, bk=bk, c=c: h.tensor_copy(out=XB[:, c, 0:T], in_=bk.t[:, 0:T]), reads=[bk.b], writes=[XBb[c]])
                ps_free(bk)

    def store_y(tl):
        T, G, npp = tl.T, tl.G, tl.np
        with Phase() as ph:
            YT, YTb = ph.sb("YT", [128, G, D_MODEL], F32)
            for g in range(G):
                for q in range(4):
                    bk = ps_alloc()
                    fns = [mmf(bk.t[0:npp, j * 128:(j + 1) * 128], X[:, 4 * q + j, g * npp:(g + 1) * npp], IDENT, j == 0, j == 3)
                           for j in range(4)]
                    MM(fns, reads=[Xb[4 * q + j] for j in range(4)] + [CTb], writes=[bk.b])
                    if q % 2 == 0:
                        A(lambda h, bk=bk, g=g, q=q: h.activation(out=YT[0:npp, g, q * 512:(q + 1) * 512], in_=bk.t[0:npp, :], func=AF.Copy),
                          reads=[bk.b], writes=[YTb])
                    else:
                        V(lambda h, bk=bk, g=g, q=q: h.tensor_copy(out=YT[0:npp, g, q * 512:(q + 1) * 512], in_=bk.t[0:npp, :]),
                          reads=[bk.b], writes=[YTb])
                    ps_free(bk)
            if tl.kind == "p":
                Dq("sp", y_p[tl.idx * 512:(tl.idx + 1) * 512, :].rearrange("(g p) d -> p g d", p=128), YT[:], reads=[YTb])
            else:
                Dq("sp", y_s, YT[0:npp, 0, :], reads=[YTb])

    def layer_norm(tl, l, i, bm, bq):
        T = tl.T
        gi = l * 2 + i
        with Phase() as ph:
            MEAN, MEANb = ph.sb("MEAN", [128, T], F32)
            RSTD, RSTDb = ph.sb("RSTD", [128, T], F32)
            T1, T1b = ph.sb("T1", [128, 2, T], F32, nb=2)
            A(lambda h: h.activation(out=MEAN[:], in_=bm.t[:, 0:T], func=AF.Copy, scale=1.0 / D_MODEL), reads=[bm.b], writes=[MEANb])
            V(lambda h: h.tensor_tensor(out=RSTD[:], in0=MEAN[:], in1=MEAN[:], op=ALU.mult), reads=[MEANb], writes=[RSTDb])
            V(lambda h: h.scalar_tensor_tensor(out=RSTD[:], in0=bq.t[:, 0:T], scalar=1.0 / D_MODEL, in1=RSTD[:], op0=ALU.mult, op1=ALU.subtract),
              reads=[bq.b, RSTDb], writes=[RSTDb])
            A(lambda h: h.activation(out=RSTD[:], in_=RSTD[:], func=AF.Sqrt, bias=LN_EPS), reads=[RSTDb], writes=[RSTDb])
            V(lambda h: h.reciprocal(out=RSTD[:], in_=RSTD[:]), reads=[RSTDb], writes=[RSTDb])
            ps_free(bm)
            ps_free(bq)
            for c in range(16):
                V(lambda h, c=c: h.tensor_tensor(out=T1[:, c % 2, :], in0=X[:, c, 0:T], in1=MEAN[:], op=ALU.subtract),
                  reads=[Xb[c], MEANb], writes=[T1b[c % 2]])
                V(lambda h, c=c: h.tensor_tensor(out=T1[:, c % 2, :], in0=T1[:, c % 2, :], in1=RSTD[:], op=ALU.mult),
                  reads=[T1b[c % 2], RSTDb], writes=[T1b[c % 2]])
                A(lambda h, c=c: h.activation(out=X[:, c, 0:T], in_=T1[:, c % 2, :], func=AF.Identity,
                                              scale=PVL[:, c, gi:gi + 1], bias=PVL[:, c, 8 + gi:9 + gi]),
                  reads=[T1b[c % 2], PVLb], writes=[Xb[c]])
                A(lambda h, c=c: h.activation(out=XB[:, c, 0:T], in_=X[:, c, 0:T], func=AF.Copy), reads=[Xb[c]], writes=[XBb[c]])

    def proj_res(tl, wk, W2, rhs):
        T = tl.T
        bm = ps_alloc()
        bq = ps_alloc()
        with Phase() as ph:
            SQ, SQb = ph.sb("SQl", [128, 2, T], BF16, nb=2)

            def evac(bi, banks):
                for j, bk in enumerate(banks):
                    c = 4 * bi + j
                    V(lambda h, bk=bk, c=c: h.scalar_tensor_tensor(out=X[:, c, 0:T], in0=X[:, c, 0:T], scalar=ALPHA, in1=bk.t[:, 0:T],
                                                                   op0=ALU.mult, op1=ALU.add),
                      reads=[Xb[c], bk.b], writes=[Xb[c]])
                    A(lambda h, c=c: h.activation(out=XB[:, c, 0:T], in_=X[:, c, 0:T], func=AF.Copy), reads=[Xb[c]], writes=[XBb[c]])
                    A(lambda h, c=c: h.activation(out=SQ[:, c % 2, :], in_=X[:, c, 0:T], func=AF.Square), reads=[Xb[c]], writes=[SQb[c % 2]])
                    MM([mmf(bm.t[:, 0:T], ONESB[:, :], XB[:, c, 0:T], c == 0, c == 15)], reads=[XBb[c], ONESBb], writes=[bm.b])
                    MM([mmf(bq.t[:, 0:T], ONESB[:, :], SQ[:, c % 2, :], c == 0, c == 15)], reads=[SQb[c % 2], ONESBb], writes=[bq.b])

            linear_fm(wk, W2, [[(n * 512, 512)] for n in range(4)], rhs, T, evac)
        return bm, bq

    def even_mixer(tl, e, MIX, MIXb):
        T, G, npp = tl.T, tl.G, tl.np
        Wi = W["ev_w_in"][e]
        isP = tl.kind == "p"
        cdec = cdec_p if isP else cdec_s
        with Phase() as ph:
            RK, RKb = ph.sb("RK", [128, 2, T], F32)
            Dq("sp", RK[:], rot[0:2, :, tl.rc0:tl.rc0 + T].rearrange("a p t -> p a t"), writes=[RKb])
            RQ, RQb = ph.sb("RQ", [128, 2, 2, T], F32, nb=2)
            TA, TAb = ph.sb("TA", [128, 2, T], F32, nb=2)
            QR, QRb = ph.sb("QR", [128, 2, 2, T], BF16, nb=2)
            KR, KRb = ph.sb("KR", [128, 2, 2, T], BF16, nb=2)
            VV, VVb = ph.sb("VV", [128, 2, G, 256], BF16, nb=2)
            KZ, KZb = ph.sb("KZ", [128, 2, G, 256], BF16, nb=2)
            GS, GSb = ph.sb("GS", [128, 2, 2, T], BF16, nb=2)
            IT, ITb = ph.sb("IT", [128, 2, 128], BF16, nb=2)
            SBF, SBFb = ph.sb("SBF", [128, 2, 512], BF16, nb=2)
            SQ, SQb = ph.sb("SQr", [128, 2, T], F32, nb=2)
            SQH, SQHb = ph.sb("SQH", [128, 2, T], BF16, nb=2)
            RN, RNb = ph.sb("RN", [128, T], F32)
            if not isP:
                KZM, KZMb = ph.sb("KZM", [TS, NS, 256], BF16)
                SIN, SINb = ph.sb("SIN", [128, 2, 4, 512], F32, nb=2)
                SBS, SBSb = ph.sb("SBS", [128, 2, 4, 512], BF16, nb=2)
                DMASK = lambda hh: CT[0:TS, CT_DMS + hh * 64:CT_DMS + (hh + 1) * 64]
                ZETA = lambda hh: CT[0:TS, CT_ZS + hh:CT_ZS + hh + 1]
            else:
                DMASK = lambda hh: CT[:, CT_DMP + hh * 128:CT_DMP + (hh + 1) * 128]
                ZETA = lambda hh: CT[:, CT_ZP + hh:CT_ZP + hh + 1]

            for hd in range(4):
                p2 = hd % 2
                Dq("sp", RQ[:, p2, :, :], rot[2 + 2 * hd:4 + 2 * hd, :, tl.rc0:tl.rc0 + T].rearrange("a p t -> p a t"), writes=[RQb[p2]])

                def rotary(b1, b2, tab, tabb, dst, dstb, ti):
                    cs, sn = tab[:, 0, :], tab[:, 1, :]
                    V(lambda h: h.tensor_tensor(out=TA[:, 0, :], in0=b1.t[:, 0:T], in1=cs, op=ALU.mult), reads=[b1.b, tabb], writes=[TAb[0]])
                    V(lambda h: h.tensor_tensor(out=TA[:, 1, :], in0=b2.t[:, 0:T], in1=sn, op=ALU.mult), reads=[b2.b, tabb], writes=[TAb[1]])
                    V(lambda h: h.tensor_tensor(out=dst[:, 0, :], in0=TA[:, 0, :], in1=TA[:, 1, :], op=ALU.subtract), reads=[TAb[0], TAb[1]], writes=[dstb])
                    V(lambda h: h.tensor_tensor(out=TA[:, 0, :], in0=b2.t[:, 0:T], in1=cs, op=ALU.mult), reads=[b2.b, tabb], writes=[TAb[0]])
                    V(lambda h: h.tensor_tensor(out=TA[:, 1, :], in0=b1.t[:, 0:T], in1=sn, op=ALU.mult), reads=[b1.b, tabb], writes=[TAb[1]])
                    V(lambda h: h.tensor_tensor(out=dst[:, 1, :], in0=TA[:, 0, :], in1=TA[:, 1, :], op=ALU.add), reads=[TAb[0], TAb[1]], writes=[dstb])

                def evacA(bi, banks, hd=hd, p2=p2):
                    rotary(banks[0], banks[1], RQ[:, p2], RQb[p2], QR[:, p2], QRb[p2], 0)
                    rotary(banks[2], banks[3], RK, RKb, KR[:, p2], KRb[p2], 1)

                linear_fm(f"evin{e}", Wi, [[(hd * 256, 256), (1024 + hd * 256, 256)]], xb_rhs(T), T, evacA)
                slot, sbuf = wload(f"evin{e}", [(Wi, 2048 + hd * 256, 256), (Wi, 3072 + hd * 256, 256)], 0, 16)
                for g in range(G):
                    bk = ps_alloc()
                    MM([mmf(bk.t[0:npp, 0:256], XB[:, kc, g * npp:(g + 1) * npp], slot[:, kc, 0:256], kc == 0, kc == 15) for kc in range(16)],
                       reads=[sbuf] + XBb, writes=[bk.b])
                    A(lambda h, bk=bk, g=g: h.activation(out=VV[0:npp, p2, g, :], in_=bk.t[0:npp, 0:256], func=AF.Copy), reads=[bk.b], writes=[VVb[p2]])
                    ps_free(bk)
                for j in range(2):
                    bk = ps_alloc()
                    MM([mmf(bk.t[:, 0:T], slot[:, kc, 256 + j * 128:256 + (j + 1) * 128], XB[:, kc, 0:T], kc == 0, kc == 15) for kc in range(16)],
                       reads=[sbuf] + XBb, writes=[bk.b])
                    A(lambda h, bk=bk, j=j: h.activation(out=GS[:, p2, j, :], in_=bk.t[:, 0:T], func=AF.Silu), reads=[bk.b], writes=[GSb[p2]])
                    ps_free(bk)
                for g in range(G):
                    bk = ps_alloc()
                    MM([mmf(bk.t[0:npp, kc2 * 128:(kc2 + 1) * 128], KR[:, p2, kc2, g * npp:(g + 1) * npp], IDB[:, :], kc2 == 0, kc2 == 1)
                        for kc2 in range(2)], reads=[KRb[p2], IDBb], writes=[bk.b])
                    A(lambda h, bk=bk, g=g: h.activation(out=KZ[0:npp, p2, g, :], in_=bk.t[0:npp, 0:256], func=AF.Identity, scale=ZETA(hd)),
                      reads=[bk.b, CTb], writes=[KZb[p2]])
                    ps_free(bk)
                PO = [ps_alloc(), ps_alloc()]
                Sst = RS[e][:, hd, :]
                Sstb = RSb[e][hd]
                if isP:
                    A(lambda h: h.activation(out=SBF[:, 0, :], in_=Sst, func=AF.Copy), reads=[Sstb], writes=[SBFb[0]])
                    for g in range(G):
                        gc = slice(g * 128, (g + 1) * 128)
                        sp_ = g % 2
                        bi_ = ps_alloc()
                        MM([mmf(bi_.t[:, 0:128], KR[:, p2, kc2, gc], QR[:, p2, kc2, gc], kc2 == 0, kc2 == 1) for kc2 in range(2)],
                           reads=[KRb[p2], QRb[p2]], writes=[bi_.b])
                        V(lambda h, bi_=bi_, g=g: h.tensor_tensor(out=IT[:, g % 2, :], in0=bi_.t[:, 0:128], in1=DMASK(hd), op=ALU.mult),
                          reads=[bi_.b, CTb], writes=[ITb[g % 2]])
                        ps_free(bi_)
                        for vc in range(2):
                            vcs = slice(vc * 128, (vc + 1) * 128)
                            MM([mmf(PO[vc].t[:, gc], VV[:, p2, g, vcs], IT[:, g % 2, :], g == 0, False),
                                mmf(PO[vc].t[:, gc], SBF[:, sp_, vc * 128:(vc + 1) * 128], QR[:, p2, 0, gc], False, False),
                                mmf(PO[vc].t[:, gc], SBF[:, sp_, 256 + vc * 128:256 + (vc + 1) * 128], QR[:, p2, 1, gc], False, g == G - 1)],
                               reads=[VVb[p2], ITb[g % 2], SBFb[sp_], QRb[p2]], writes=[PO[vc].b])
                        bs_ = ps_alloc()
                        MM([mmf(bs_.t[:, kc2 * 256:(kc2 + 1) * 256], KZ[:, p2, g, kc2 * 128:(kc2 + 1) * 128], VV[:, p2, g, :], kc2 == 0, kc2 == 1)
                            for kc2 in range(2)], reads=[KZb[p2], VVb[p2]], writes=[bs_.b])
                        V(lambda h, bs_=bs_: h.scalar_tensor_tensor(out=Sst, in0=Sst, scalar=cdec[hd], in1=bs_.t[:, :], op0=ALU.mult, op1=ALU.add),
                          reads=[Sstb, bs_.b], writes=[Sstb])
                        ps_free(bs_)
                        if g < G - 1:
                            A(lambda h, g=g: h.activation(out=SBF[:, (g + 1) % 2, :], in_=Sst, func=AF.Copy), reads=[Sstb], writes=[SBFb[(g + 1) % 2]])
                else:
                    bi_ = ps_alloc()
                    MM([mmf(bi_.t[0:TS, 0:TS], KR[:, p2, kc2, 0:TS], QR[:, p2, kc2, 0:TS], kc2 == 0, kc2 == 1) for kc2 in range(2)],
                       reads=[KRb[p2], QRb[p2]], writes=[bi_.b])
                    V(lambda h, bi_=bi_: h.tensor_tensor(out=IT[0:TS, 0, 0:TS], in0=bi_.t[0:TS, 0:TS], in1=DMASK(hd), op=ALU.mult),
                      reads=[bi_.b, CTb], writes=[ITb[0]])
                    ps_free(bi_)
                    V(lambda h: h.tensor_tensor(out=KZM[:, :, :],
                                                in0=KZ[0:TS, p2, 0, :].unsqueeze(1).to_broadcast([TS, NS, 256]),
                                                in1=CT[0:TS, CT_SQM:CT_SQM + NS].unsqueeze(2).to_broadcast([TS, NS, 256]), op=ALU.mult),
                      reads=[KZb[p2], CTb], writes=[KZMb])
                    for vc in range(2):
                        vcs = slice(vc * 128, (vc + 1) * 128)
                        MM([mmf(PO[vc].t[:, 0:TS], VV[0:TS, p2, 0, vcs], IT[0:TS, 0, 0:TS], True, False)],
                           reads=[VVb[p2], ITb[0]], writes=[PO[vc].b])
                    for q4 in range(4):
                        s2 = q4 % 2
                        for bb in range(4):
                            src = s_ret[e, 4 * q4 + bb, hd].rearrange("(c p) v -> p c v", p=128)
                            Dq("sp", SIN[:, s2, bb].rearrange("p (c v) -> p c v", c=2), src, writes=[SINb[s2]], append=(bb > 0))
                        A(lambda h, s2=s2: h.activation(out=SBS[:, s2], in_=SIN[:, s2], func=AF.Copy), reads=[SINb[s2]], writes=[SBSb[s2]])
                        for bb in range(4):
                            b = 4 * q4 + bb
                            tc_ = slice(4 * b, 4 * b + 4)
                            last = (q4 == 3 and bb == 3)
                            for vc in range(2):
                                MM([mmf(PO[vc].t[:, tc_], SBS[:, s2, bb, vc * 128:(vc + 1) * 128], QR[:, p2, 0, tc_], False, False),
                                    mmf(PO[vc].t[:, tc_], SBS[:, s2, bb, 256 + vc * 128:256 + (vc + 1) * 128], QR[:, p2, 1, tc_], False, last)],
                                   reads=[SBSb[s2], QRb[p2]], writes=[PO[vc].b])
                            bs_ = ps_alloc()
                            MM([mmf(bs_.t[:, kc2 * 256:(kc2 + 1) * 256], KZM[:, b, kc2 * 128:(kc2 + 1) * 128], VV[0:TS, p2, 0, :], kc2 == 0, kc2 == 1)
                                for kc2 in range(2)], reads=[KZMb, VVb[p2]], writes=[bs_.b])
                            V(lambda h, bs_=bs_, bb=bb, s2=s2: h.scalar_tensor_tensor(out=SIN[:, s2, bb, :], in0=SIN[:, s2, bb, :], scalar=cdec[hd],
                                                                                    in1=bs_.t[:, :], op0=ALU.mult, op1=ALU.add),
                              reads=[SINb[s2], bs_.b], writes=[SINb[s2]])
                            ps_free(bs_)
                        for bb in range(4):
                            dst = ret_s[e, 4 * q4 + bb, hd].rearrange("(c p) v -> p c v", p=128)
                            Dq("sp", dst, SIN[:, s2, bb].rearrange("p (c v) -> p c v", c=2), reads=[SINb[s2]])
                bn = ps_alloc()
                for vc in range(2):
                    A(lambda h, vc=vc: h.activation(out=SQH[:, vc, :], in_=PO[vc].t[:, 0:T], func=AF.Square), reads=[PO[vc].b], writes=[SQHb[vc]])
                    MM([mmf(bn.t[:, 0:T], ONESB[:, :], SQH[:, vc, :], vc == 0, vc == 1)], reads=[SQHb[vc], ONESBb], writes=[bn.b])
                A(lambda h: h.activation(out=RN[:], in_=bn.t[:, 0:T], func=AF.Sqrt, scale=1.0 / 256.0, bias=NORM_EPS), reads=[bn.b], writes=[RNb])
                V(lambda h: h.reciprocal(out=RN[:], in_=RN[:]), reads=[RNb], writes=[RNb])
                ps_free(bn)
                for vc in range(2):
                    V(lambda h, vc=vc: h.tensor_tensor(out=SQ[:, vc, :], in0=PO[vc].t[:, 0:T], in1=RN[:], op=ALU.mult),
                      reads=[PO[vc].b, RNb], writes=[SQb[vc]])
                    V(lambda h, vc=vc: h.tensor_tensor(out=MIX[:, 2 * hd + vc, 0:T], in0=SQ[:, vc, :], in1=GS[:, p2, vc, :], op=ALU.mult),
                      reads=[SQb[vc], GSb[p2]], writes=[MIXb[2 * hd + vc]])
                ps_free(PO[0])
                ps_free(PO[1])

        with Phase() as ph:
            WA, WAb = ph.sb("WA", [128, 8, 128], BF16)
            WX, WXb = ph.sb("WX", [128, 8, 128], BF16)
            Dq("pool", WA[:], W["ev_rg_wa"][e].rearrange("n i j -> i n j"), writes=[WAb])
            Dq("pool", WX[:], W["ev_rg_wx"][e].rearrange("n i j -> i n j"), writes=[WXb])
            NB = 2
            if isP:
                XRP, XRPb = ph.sb("XRP", [128, NB, T + 3], F32, nb=NB)
            else:
                XRP, XRPb = ph.sb("XRP", [128, NB, NS, 7], F32, nb=NB)
                SRC, SRCb = ph.sb("SRC", [128, 8, NS * 3], F32)
                H0, H0b = ph.sb("H0", [128, 8, NS], F32)
                HL, HLb = ph.sb("HL", [128, 8, NS], F32)
                OT, OTb = ph.sb("OT", [128, 8, NS * 3], F32)
                TMPS, TMPSb = ph.sb("TMPS", [128, NS], F32)
                with Phase() as ph2:
                    r1, r1b = ph2.sb("r1", [NS * 3, 1024], F32)
                    r2, r2b = ph2.sb("r2", [NS, 1024], F32)
                    Dq("sp", r1[:], s_rgc[e], writes=[r1b])
                    Dq("sp", r2[:], s_rgh[e], writes=[r2b])
                    for c in range(8):
                        bk = ps_alloc()
                        MM([mmf(bk.t[:, 0:48], r1[0:48, c * 128:(c + 1) * 128], IDENT[0:48, 0:48], True, False),
                            mmf(bk.t[:, 64:80], r2[0:16, c * 128:(c + 1) * 128], IDENT[0:16, 0:16], False, True)],
                           reads=[r1b, r2b, CTb], writes=[bk.b])
                        A(lambda h, bk=bk, c=c: h.activation(out=SRC[:, c, :], in_=bk.t[:, 0:48], func=AF.Copy), reads=[bk.b], writes=[SRCb])
                        A(lambda h, bk=bk, c=c: h.activation(out=H0[:, c, :], in_=bk.t[:, 64:80], func=AF.Copy), reads=[bk.b], writes=[H0b])
                        ps_free(bk)
            GG, GGb = ph.sb("GG", [128, NB, T], F32, nb=NB)
            XC, XCb_ = ph.sb("XC", [128, NB, T], F32, nb=NB)
            XCB, XCBb = ph.sb("XCB", [128, NB, T], BF16, nb=NB)
            RG, RGb = ph.sb("RG", [128, NB, T], F32, nb=NB)
            IG, IGb = ph.sb("IG", [128, NB, T], F32, nb=NB)
            AA, AAb = ph.sb("AA", [128, NB, T], F32, nb=NB)
            MU, MUb = ph.sb("MU", [128, NB, T], F32, nb=NB)
            HH, HHb = ph.sb("HH", [128, NB, T], F32, nb=NB)

            def v3(ap, j0, j1, w):
                return ap.rearrange("p (b j) -> p b j", j=w)[:, :, j0:j1]

            def rg_chunk(c, bx, bg):
                s = c % NB
                pw = lambda j: PVE[:, e, c, j:j + 1]
                A(lambda h: h.activation(out=GG[:, s, :], in_=bg.t[:, 0:T], func=AF.Gelu_apprx_tanh), reads=[bg.b], writes=[GGb[s]])
                if isP:
                    A(lambda h: h.activation(out=XRP[:, s, 3:3 + T], in_=bx.t[:, 0:T], func=AF.Copy), reads=[bx.b], writes=[XRPb[s]])
                    V(lambda h: h.tensor_copy(out=XRP[:, s, 0:3], in_=RGT[e][0][:, c, :]), reads=[RGT[e][1]], writes=[XRPb[s]])
                    V(lambda h: h.tensor_copy(out=RGT[e][0][:, c, :], in_=XRP[:, s, T:T + 3]), reads=[XRPb[s]], writes=[RGT[e][1]])
                    xin = lambda j: XRP[:, s, j:j + T]
                    xc = XC[:, s, :]
                else:
                    A(lambda h: h.activation(out=XRP[:, s, :, 3:7], in_=v3(bx.t[:, 0:T], 0, 4, 4), func=AF.Copy), reads=[bx.b], writes=[XRPb[s]])
                    V(lambda h: h.tensor_copy(out=XRP[:, s, :, 0:3], in_=v3(SRC[:, c, :], 0, 3, 3)), reads=[SRCb], writes=[XRPb[s]])
                    V(lambda h: h.tensor_copy(out=v3(OT[:, c, :], 0, 3, 3), in_=XRP[:, s, :, 4:7]), reads=[XRPb[s]], writes=[OTb])
                    xin = lambda j: XRP[:, s, :, j:j + 4]
                    xc = v3(XC[:, s, :], 0, 4, 4)
                A(lambda h: h.activation(out=xc, in_=xin(0), func=AF.Identity, scale=pw(0), bias=pw(4)), reads=[XRPb[s], PVEb], writes=[XCb_[s]])
                for j in (1, 2, 3):
                    V(lambda h, j=j: h.scalar_tensor_tensor(out=xc, in0=xin(j), scalar=pw(j), in1=xc, op0=ALU.mult, op1=ALU.add),
                      reads=[XRPb[s], PVEb, XCb_[s]], writes=[XCb_[s]])
                A(lambda h: h.activation(out=XCB[:, s, :], in_=XC[:, s, :], func=AF.Copy), reads=[XCb_[s]], writes=[XCBb[s]])
                br = ps_alloc()
                MM([mmf(br.t[:, 0:T], WA[:, c, :], XCB[:, s, :], True, True)], reads=[WAb, XCBb[s]], writes=[br.b])
                A(lambda h: h.activation(out=RG[:, s, :], in_=br.t[:, 0:T], func=AF.Sigmoid, bias=pw(5)), reads=[br.b, PVEb], writes=[RGb[s]])
                ps_free(br)
                bi_ = ps_alloc()
                MM([mmf(bi_.t[:, 0:T], WX[:, c, :], XCB[:, s, :], True, True)], reads=[WXb, XCBb[s]], writes=[bi_.b])
                A(lambda h: h.activation(out=IG[:, s, :], in_=bi_.t[:, 0:T], func=AF.Sigmoid, bias=pw(6)), reads=[bi_.b, PVEb], writes=[IGb[s]])
                ps_free(bi_)
                A(lambda h: h.activation(out=AA[:, s, :], in_=RG[:, s, :], func=AF.Exp, scale=C8[:, e, 0, c:c + 1]), reads=[RGb[s], C8b], writes=[AAb[s]])
                A(lambda h: h.activation(out=MU[:, s, :], in_=RG[:, s, :], func=AF.Exp, scale=C8[:, e, 1, c:c + 1]), reads=[RGb[s], C8b], writes=[MUb[s]])
                V(lambda h: h.tensor_scalar(out=MU[:, s, :], in0=MU[:, s, :], scalar1=1.0, scalar2=None, op0=ALU.min), reads=[MUb[s]], writes=[MUb[s]])
                A(lambda h: h.activation(out=MU[:, s, :], in_=MU[:, s, :], func=AF.Sqrt, scale=-1.0, bias=1.0), reads=[MUb[s]], writes=[MUb[s]])
                if isP and tl.idx == 0:
                    V(lambda h: h.memset(MU[:, s, 0:1], 1.0), writes=[MUb[s]])
                V(lambda h: h.tensor_tensor(out=IG[:, s, :], in0=IG[:, s, :], in1=XC[:, s, :], op=ALU.mult), reads=[IGb[s], XCb_[s]], writes=[IGb[s]])
                V(lambda h: h.tensor_tensor(out=IG[:, s, :], in0=IG[:, s, :], in1=MU[:, s, :], op=ALU.mult), reads=[IGb[s], MUb[s]], writes=[IGb[s]])
                if isP:
                    V(lambda h: h.tensor_tensor_scan(out=HH[:, s, :], data0=AA[:, s, :], data1=IG[:, s, :], initial=RGH[e][0][:, c:c + 1],
                                                     op0=ALU.mult, op1=ALU.add),
                      reads=[AAb[s], IGb[s], RGH[e][1]], writes=[HHb[s]])
                    V(lambda h: h.tensor_copy(out=RGH[e][0][:, c:c + 1], in_=HH[:, s, T - 1:T]), reads=[HHb[s]], writes=[RGH[e][1]])
                else:
                    a3 = v3(AA[:, s, :], 0, 4, 4)
                    b3 = v3(IG[:, s, :], 0, 4, 4)
                    h3 = v3(HH[:, s, :], 0, 4, 4)
                    for j in range(4):
                        prev = H0[:, c, :] if j == 0 else h3[:, :, j - 1]
                        V(lambda h, j=j, prev=prev: h.tensor_tensor(out=TMPS[:, :], in0=a3[:, :, j], in1=prev, op=ALU.mult),
                          reads=[AAb[s], H0b, HHb[s]], writes=[TMPSb])
                        V(lambda h, j=j: h.tensor_tensor(out=h3[:, :, j], in0=TMPS[:, :], in1=b3[:, :, j], op=ALU.add),
                          reads=[TMPSb, IGb[s]], writes=[HHb[s]])
                    V(lambda h: h.tensor_copy(out=HL[:, c, :], in_=h3[:, :, 3]), reads=[HHb[s]], writes=[HLb])
                V(lambda h: h.tensor_tensor(out=MIX[:, 8 + c, 0:T], in0=HH[:, s, :], in1=GG[:, s, :], op=ALU.mult),
                  reads=[HHb[s], GGb[s]], writes=[MIXb[8 + c]])

            def evacR(bi, banks):
                for jj in range(2):
                    rg_chunk(2 * bi + jj, banks[jj], banks[2 + jj])

            linear_fm(f"evin{e}", Wi, [[(4096 + 256 * i, 256), (5120 + 256 * i, 256)] for i in range(4)], xb_rhs(T), T, evacR)

            if not isP:
                with Phase() as ph2:
                    o1, o1b = ph2.sb("o1", [NS * 3, 1024], F32)
                    o2, o2b = ph2.sb("o2", [NS, 1024], F32)
                    for q in range(2):
                        bk = ps_alloc()
                        MM([mmf(bk.t[0:48, j * 128:(j + 1) * 128], OT[:, 4 * q + j, :], IDENT, j == 0, j == 3) for j in range(4)],
                           reads=[OTb, CTb], writes=[bk.b])
                        A(lambda h, bk=bk, q=q: h.activation(out=o1[:, q * 512:(q + 1) * 512], in_=bk.t[0:48, :], func=AF.Copy), reads=[bk.b], writes=[o1b])
                        ps_free(bk)
                        bk = ps_alloc()
                        MM([mmf(bk.t[0:16, j * 128:(j + 1) * 128], HL[:, 4 * q + j, :], IDENT, j == 0, j == 3) for j in range(4)],
                           reads=[HLb, CTb], writes=[bk.b])
                        A(lambda h, bk=bk, q=q: h.activation(out=o2[:, q * 512:(q + 1) * 512], in_=bk.t[0:16, :], func=AF.Copy), reads=[bk.b], writes=[o2b])
                        ps_free(bk)
                    Dq("sp", rgc_s[e], o1[:], reads=[o1b])
                    Dq("sp", rgh_s[e], o2[:], reads=[o2b])

    def odd_mixer(tl, o, MIX, MIXb):
        T, G, npp, C = tl.T, tl.G, tl.np, tl.C
        NCH = T // C
        mid = C // 2 - 1
        Wi = W["od_w_in"][o]
        isP = tl.kind == "p"
        RM = RMP if isP else RMS
        RMb = RMPb if isP else RMSb
        with Phase() as ph:
            NB = 2
            QS, QSb = ph.sb("QS", [128, NB, T], F32, nb=NB)
            SG, SGb = ph.sb("SG", [128, NB, T], F32, nb=NB)
            KK, KKb = ph.sb("KK", [128, NB, T], F32, nb=NB)
            BC, BCb = ph.sb("BC", [128, NB, T], F32, nb=NB)
            D1, D1b = ph.sb("D1", [128, NB, T], F32, nb=NB)
            E1, E1b = ph.sb("E1", [128, NB, T], F32, nb=NB)
            QH, QHb = ph.sb("QH", [128, NB, T], BF16, nb=NB)
            KH, KHb = ph.sb("KH", [128, NB, T], BF16, nb=NB)
            QC, QCb = ph.sb("QC", [128, NB, T], F32 if isP else BF16, nb=NB)
            KT, KTb = ph.sb("KT", [128, NB, T], BF16, nb=NB)
            EBL, EBLb = ph.sb("EBL", [128, NB, NCH], F32, nb=NB)
            SGG, SGGb = ph.sb("SGG", [128, NB, T], BF16, nb=NB)
            VT, VTb = ph.sb("VT", [128, G, 256], BF16)
            ATM, ATMb = ph.sb("ATM", [128, 2, 128], BF16, nb=2)
            HS2, HS2b = ph.sb("HS2", [128, NB, 128], F32, nb=NB)
            LF, LFb = SG, SGb
            SQ, SQb = D1, D1b
            RN, RNb = E1, E1b
            SQB, SQBb = ph.sb("SQBo", [128, NB, T], BF16, nb=NB)
            if isP:
                KTM, KTMb = ph.sb("KTM", [128, NB, 2, 4, 128], BF16, nb=NB * 2)
            else:
                KTM, KTMb = ph.sb("KTM", [TS, NB, NS, 128], BF16, nb=NB)
                SIN, SINb = ph.sb("SINh", [128, NB, NS, 128], F32, nb=NB)
                SOUT, SOUTb = ph.sb("SOUT", [128, NB, NS, 128], F32, nb=NB)
                SB16, SB16b = ph.sb("SB16", [128, NB, NS, 128], BF16, nb=NB)

            for hp in range(8):
                hds = (2 * hp, 2 * hp + 1)

                def evacA(bi, banks, hds=hds):
                    for hh in range(2):
                        hd = hds[hh]
                        bq, bf = banks[hh], banks[2 + hh]
                        lb = LBT[:, o, 0, hd:hd + 1]
                        oml = LBT[:, o, 1, hd:hd + 1]
                        noml = LBT[:, o, 2, hd:hd + 1]
                        A(lambda h, hh=hh, bq=bq: h.activation(out=QS[:, hh, :], in_=bq.t[:, 0:T], func=AF.Silu), reads=[bq.b], writes=[QSb[hh]])
                        A(lambda h, hh=hh, bf=bf: h.activation(out=SG[:, hh, :], in_=bf.t[:, 0:T], func=AF.Sigmoid), reads=[bf.b], writes=[SGb[hh]])
                        V(lambda h, hh=hh, oml=oml, noml=noml: h.tensor_scalar(out=KK[:, hh, :], in0=SG[:, hh, :], scalar1=noml, scalar2=oml,
                                                                                op0=ALU.mult, op1=ALU.add),
                          reads=[SGb[hh], LBTb], writes=[KKb[hh]])
                        A(lambda h, hh=hh, lb=lb, oml=oml: h.activation(out=LF[:, hh, :], in_=SG[:, hh, :], func=AF.Ln, scale=oml, bias=lb),
                          reads=[SGb[hh], LBTb], writes=[LFb[hh]])
                        V(lambda h, hh=hh: h.tensor_tensor_scan(out=BC[:, hh, :], data0=RM[:, 0:T], data1=LF[:, hh, :], initial=0.0,
                                                                op0=ALU.mult, op1=ALU.add),
                          reads=[RMb, LFb[hh]], writes=[BCb[hh]])
                        B3 = BC[:, hh, :].rearrange("p (n j) -> p n j", j=C)
                        d3 = D1[:, hh, :].rearrange("p (n j) -> p n j", j=C)
                        V(lambda h, B3=B3, d3=d3: h.tensor_tensor(out=d3, in0=B3, in1=B3[:, :, mid:mid + 1].to_broadcast([128, NCH, C]), op=ALU.subtract),
                          reads=[BCb[hh]], writes=[D1b[hh]])
                        A(lambda h, hh=hh: h.activation(out=E1[:, hh, :], in_=D1[:, hh, :], func=AF.Exp), reads=[D1b[hh]], writes=[E1b[hh]])
                        A(lambda h, hh=hh: h.activation(out=D1[:, hh, :], in_=D1[:, hh, :], func=AF.Exp, scale=-1.0), reads=[D1b[hh]], writes=[D1b[hh]])
                        V(lambda h, hh=hh: h.tensor_tensor(out=QH[:, hh, :], in0=QS[:, hh, :], in1=E1[:, hh, :], op=ALU.mult),
                          reads=[QSb[hh], E1b[hh]], writes=[QHb[hh]])
                        V(lambda h, hh=hh: h.tensor_tensor(out=KH[:, hh, :], in0=KK[:, hh, :], in1=D1[:, hh, :], op=ALU.mult),
                          reads=[KKb[hh], D1b[hh]], writes=[KHb[hh]])
                        A(lambda h, hh=hh: h.activation(out=E1[:, hh, :], in_=BC[:, hh, :], func=AF.Exp), reads=[BCb[hh]], writes=[E1b[hh]])
                        V(lambda h, B3=B3, d3=d3, hh=hh: h.tensor_tensor(out=d3, in0=B3[:, :, C - 1:C].to_broadcast([128, NCH, C]), in1=B3, op=ALU.subtract),
                          reads=[BCb[hh]], writes=[D1b[hh]])
                        A(lambda h, hh=hh: h.activation(out=D1[:, hh, :], in_=D1[:, hh, :], func=AF.Exp), reads=[D1b[hh]], writes=[D1b[hh]])
                        V(lambda h, hh=hh: h.tensor_tensor(out=QC[:, hh, :], in0=QS[:, hh, :], in1=E1[:, hh, :], op=ALU.mult),
                          reads=[QSb[hh], E1b[hh]], writes=[QCb[hh]])
                        V(lambda h, hh=hh: h.tensor_tensor(out=KT[:, hh, :], in0=KK[:, hh, :], in1=D1[:, hh, :], op=ALU.mult),
                          reads=[KKb[hh], D1b[hh]], writes=[KTb[hh]])
                        A(lambda h, hh=hh, B3=B3: h.activation(out=EBL[:, hh, :], in_=B3[:, :, C - 1], func=AF.Exp), reads=[BCb[hh]], writes=[EBLb[hh]])

                linear_fm(f"odin{o}", Wi, [[(hp * 256, 256), (2048 + hp * 256, 256)]], xb_rhs(T), T, evacA)
                slot, sbuf = wload(f"odin{o}", [(Wi, 4096 + hp * 256, 256), (Wi, 6144 + hp * 256, 256)], 0, 16)
                for g in range(G):
                    bk = ps_alloc()
                    MM([mmf(bk.t[0:npp, 0:256], XB[:, kc, g * npp:(g + 1) * npp], slot[:, kc, 0:256], kc == 0, kc == 15) for kc in range(16)],
                       reads=[sbuf] + XBb, writes=[bk.b])
                    A(lambda h, bk=bk, g=g: h.activation(out=VT[0:npp, g, :], in_=bk.t[0:npp, 0:256], func=AF.Copy), reads=[bk.b], writes=[VTb])
                    ps_free(bk)
                for hh in range(2):
                    bk = ps_alloc()
                    MM([mmf(bk.t[:, 0:T], slot[:, kc, 256 + hh * 128:256 + (hh + 1) * 128], XB[:, kc, 0:T], kc == 0, kc == 15) for kc in range(16)],
                       reads=[sbuf] + XBb, writes=[bk.b])
                    A(lambda h, bk=bk, hh=hh: h.activation(out=SGG[:, hh, :], in_=bk.t[:, 0:T], func=AF.Sigmoid), reads=[bk.b], writes=[SGGb[hh]])
                    ps_free(bk)
                def ktm_build(hh, g):
                    bk = ps_alloc()
                    MM([mmf(bk.t[0:npp, 0:128], KT[:, hh, g * npp:(g + 1) * npp], IDB[:, :], True, True)], reads=[KTb[hh], IDBb], writes=[bk.b])
                    if isP:
                        V(lambda h: h.tensor_tensor(
                            out=KTM[:, hh, g % 2, :, :], in0=bk.t[:, 0:128].unsqueeze(1).to_broadcast([128, 4, 128]),
                            in1=CT[:, CT_CM:CT_CM + 4].unsqueeze(2).to_broadcast([128, 4, 128]), op=ALU.mult),
                          reads=[bk.b, CTb], writes=[KTMb[hh * 2 + g % 2]])
                    else:
                        V(lambda h: h.tensor_tensor(
                            out=KTM[:, hh, :, :], in0=bk.t[0:TS, 0:128].unsqueeze(1).to_broadcast([TS, NS, 128]),
                            in1=CT[0:TS, CT_SQM:CT_SQM + NS].unsqueeze(2).to_broadcast([TS, NS, 128]), op=ALU.mult),
                          reads=[bk.b, CTb], writes=[KTMb[hh]])
                    ps_free(bk)
                if not isP:
                    for hh in range(2):
                        ktm_build(hh, 0)
                PO = [ps_alloc(), ps_alloc()]
                if isP:
                    def st_ap(hh, par):
                        return HS[o][:, hds[hh], :] if par == 0 else HS2[:, hh, :]

                    def st_buf(hh, par):
                        return HSb[o][hds[hh]] if par == 0 else HS2b[hh]

                    for hh in range(2):
                        ktm_build(hh, 0)
                    for g in range(G):
                        gc = slice(g * 128, (g + 1) * 128)
                        for hh in range(2):
                            ba = ps_alloc()
                            MM([mmf(ba.t[:, 0:128], KH[:, hh, gc], QH[:, hh, gc], True, True)], reads=[KHb[hh], QHb[hh]], writes=[ba.b])
                            V(lambda h, ba=ba, hh=hh: h.tensor_tensor(out=ATM[:, hh, :], in0=ba.t[:, 0:128], in1=CT[:, CT_BCM:CT_BCM + 128], op=ALU.mult),
                              reads=[ba.b, CTb], writes=[ATMb[hh]])
                            ps_free(ba)
                            MM([mmf(PO[hh].t[:, gc], VT[:, g, hh * 128:(hh + 1) * 128], ATM[:, hh, :], g == 0, False)],
                               reads=[VTb, ATMb[hh]], writes=[PO[hh].b])
                        if g + 1 < G:
                            for hh in range(2):
                                ktm_build(hh, g + 1)
                        for j in range(4):
                            n = 4 * g + j
                            ncs = slice(n * 32, (n + 1) * 32)
                            for hh in range(2):
                                hd = hds[hh]
                                par = n % 2
                                MM([mmf(PO[hh].t[:, ncs], st_ap(hh, par), QC[:, hh, ncs], False, n == NCH - 1)],
                                   reads=[st_buf(hh, par), QCb[hh]], writes=[PO[hh].b])
                                bs_ = ps_alloc()
                                MM([mmf(bs_.t[:, 0:128], KTM[:, hh, g % 2, j, :], VT[:, g, hh * 128:(hh + 1) * 128], True, True)],
                                   reads=[KTMb[hh * 2 + g % 2], VTb], writes=[bs_.b])
                                V(lambda h, bs_=bs_, hh=hh, n=n, par=par: h.scalar_tensor_tensor(
                                    out=st_ap(hh, 1 - par), in0=st_ap(hh, par), scalar=EBL[:, hh, n:n + 1], in1=bs_.t[:, 0:128], op0=ALU.mult, op1=ALU.add),
                                  reads=[st_buf(hh, par), EBLb[hh], bs_.b], writes=[st_buf(hh, 1 - par)])
                                ps_free(bs_)
                else:
                    for hh in range(2):
                        hd = hds[hh]
                        Dq("sp", SIN[:, hh], s_hg[o, :, hd].rearrange("b k v -> k b v"), writes=[SINb[hh]])
                        A(lambda h, hh=hh: h.activation(out=SB16[:, hh], in_=SIN[:, hh], func=AF.Copy), reads=[SINb[hh]], writes=[SB16b[hh]])
                        ba = ps_alloc()
                        MM([mmf(ba.t[0:TS, 0:TS], KH[:, hh, 0:TS], QH[:, hh, 0:TS], True, True)], reads=[KHb[hh], QHb[hh]], writes=[ba.b])
                        V(lambda h, ba=ba, hh=hh: h.tensor_tensor(out=ATM[0:TS, hh, 0:TS], in0=ba.t[0:TS, 0:TS], in1=CT[0:TS, CT_BCS:CT_BCS + TS], op=ALU.mult),
                          reads=[ba.b, CTb], writes=[ATMb[hh]])
                        ps_free(ba)
                        fns = [mmf(PO[hh].t[:, 0:TS], VT[0:TS, 0, hh * 128:(hh + 1) * 128], ATM[0:TS, hh, 0:TS], True, False)]
                        for b in range(NS):
                            fns.append(mmf(PO[hh].t[:, 4 * b:4 * b + 4], SB16[:, hh, b, :], QC[:, hh, 4 * b:4 * b + 4], False, b == NS - 1))
                        MM(fns, reads=[VTb, ATMb[hh], SB16b[hh], QCb[hh]], writes=[PO[hh].b])
                        for q4 in range(4):
                            bs_ = ps_alloc()
                            MM([mmf(bs_.t[:, bb * 128:(bb + 1) * 128], KTM[:, hh, 4 * q4 + bb, :], VT[0:TS, 0, hh * 128:(hh + 1) * 128], bb == 0, bb == 3)
                                for bb in range(4)], reads=[KTMb[hh], VTb], writes=[bs_.b])
                            V(lambda h, hh=hh, q4=q4: h.tensor_tensor(
                                out=SOUT[:, hh, 4 * q4:4 * q4 + 4, :], in0=SIN[:, hh, 4 * q4:4 * q4 + 4, :],
                                in1=EBL[:, hh, 4 * q4:4 * q4 + 4].unsqueeze(2).to_broadcast([128, 4, 128]), op=ALU.mult),
                              reads=[SINb[hh], EBLb[hh]], writes=[SOUTb[hh]])
                            V(lambda h, hh=hh, q4=q4, bs_=bs_: h.tensor_tensor(
                                out=SOUT[:, hh, 4 * q4:4 * q4 + 4, :], in0=SOUT[:, hh, 4 * q4:4 * q4 + 4, :],
                                in1=bs_.t[:, :].rearrange("p (b v) -> p b v", v=128), op=ALU.add),
                              reads=[SOUTb[hh], bs_.b], writes=[SOUTb[hh]])
                            ps_free(bs_)
                        Dq("sp", hg_s[o, :, hd].rearrange("b k v -> k b v"), SOUT[:, hh], reads=[SOUTb[hh]])
                for hh in range(2):
                    hd = hds[hh]
                    A(lambda h, hh=hh: h.activation(out=SQB[:, hh, :], in_=PO[hh].t[:, 0:T], func=AF.Square), reads=[PO[hh].b], writes=[SQBb[hh]])
                    bn = ps_alloc()
                    MM([mmf(bn.t[:, 0:T], ONESB[:, :], SQB[:, hh, :], True, True)], reads=[SQBb[hh], ONESBb], writes=[bn.b])
                    A(lambda h, hh=hh, bn=bn: h.activation(out=RN[:, hh, :], in_=bn.t[:, 0:T], func=AF.Sqrt, scale=1.0 / 128.0, bias=NORM_EPS),
                      reads=[bn.b], writes=[RNb[hh]])
                    ps_free(bn)
                    V(lambda h, hh=hh: h.reciprocal(out=RN[:, hh, :], in_=RN[:, hh, :]), reads=[RNb[hh]], writes=[RNb[hh]])
                    V(lambda h, hh=hh: h.tensor_tensor(out=SQ[:, hh, :], in0=PO[hh].t[:, 0:T], in1=RN[:, hh, :], op=ALU.mult),
                      reads=[PO[hh].b, RNb[hh]], writes=[SQb[hh]])
                    V(lambda h, hh=hh, hd=hd: h.scalar_tensor_tensor(out=MIX[:, hd, 0:T], in0=SQ[:, hh, :], scalar=PVO[:, 0, 2 + o:3 + o],
                                                                     in1=SGG[:, hh, :], op0=ALU.mult, op1=ALU.mult),
                      reads=[SQb[hh], PVOb, SGGb[hh]], writes=[MIXb[hd]])
                ps_free(PO[0])
                ps_free(PO[1])

    def ffn(tl, l):
        T, G, npp = tl.T, tl.G, tl.np
        isP = tl.kind == "p"
        Wu = W["ffn_w_up"][l]
        Wd = W["ffn_w_down"][l]
        with Phase() as ph:
            HID, HIDb = ph.sb("HID", [128, NFC, T], BF16, nb=NFC)
            NB = 3
            if isP:
                UP, UPb = ph.sb("UP", [128, NB, T + 2], F32, nb=NB)
            else:
                UP, UPb = ph.sb("UP", [128, NB, NS, 6], F32, nb=NB)
                SF, SFb = ph.sb("SF", [NS * 2, D_FF], F32)
                FO, FOb = ph.sb("FO", [NS * 2, D_FF], F32)
                Dq("sp", SF[:], s_ffc[l], writes=[SFb])
                UO, UOb = ph.sb("UO", [128, 2, NS * 2], F32, nb=2)
            UC, UCb = ph.sb("UC", [128, 2, T], F32, nb=2)
            GU, GUb = ph.sb("GU", [128, 2, T], BF16, nb=2)
            cnt = [0]

            def v3(ap, j0, j1, w):
                return ap.rearrange("p (b j) -> p b j", j=w)[:, :, j0:j1]

            def chunk(cc, bu, bv):
                s = cnt[0] % NB
                s2 = cnt[0] % 2
                cnt[0] += 1
                pw = lambda j: PVF[:, cc, 4 * l + j:4 * l + j + 1]
                if isP:
                    A(lambda h: h.activation(out=UP[:, s, 2:2 + T], in_=bu.t[:, 0:T], func=AF.Copy), reads=[bu.b], writes=[UPb[s]])
                    V(lambda h: h.tensor_copy(out=UP[:, s, 0:2], in_=FTL[l][0][:, cc, :]), reads=[FTL[l][1]], writes=[UPb[s]])
                    V(lambda h: h.tensor_copy(out=FTL[l][0][:, cc, :], in_=UP[:, s, T:T + 2]), reads=[UPb[s]], writes=[FTL[l][1]])
                    uin = lambda j: UP[:, s, j:j + T]
                    uc = UC[:, s2, :]
                else:
                    A(lambda h: h.activation(out=UP[:, s, :, 2:6], in_=v3(bu.t[:, 0:T], 0, 4, 4), func=AF.Copy), reads=[bu.b], writes=[UPb[s]])
                    bh = ps_alloc()
                    MM([mmf(bh.t[:, 0:32], SF[0:32, cc * 128:(cc + 1) * 128], IDENT[0:32, 0:32], True, True)], reads=[SFb, CTb], writes=[bh.b])
                    V(lambda h: h.tensor_copy(out=UP[:, s, :, 0:2], in_=v3(bh.t[:, 0:32], 0, 2, 2)), reads=[bh.b], writes=[UPb[s]])
                    ps_free(bh)
                    V(lambda h: h.tensor_copy(out=v3(UO[:, s2, :], 0, 2, 2), in_=UP[:, s, :, 4:6]), reads=[UPb[s]], writes=[UOb[s2]])
                    bo = ps_alloc()
                    MM([mmf(bo.t[0:32, 0:128], UO[:, s2, :], IDENT, True, True)], reads=[UOb[s2], CTb], writes=[bo.b])
                    A(lambda h: h.activation(out=FO[:, cc * 128:(cc + 1) * 128], in_=bo.t[0:32, 0:128], func=AF.Copy), reads=[bo.b], writes=[FOb])
                    ps_free(bo)
                    uin = lambda j: UP[:, s, :, j:j + 4]
                    uc = v3(UC[:, s2, :], 0, 4, 4)
                A(lambda h: h.activation(out=uc, in_=uin(0), func=AF.Identity, scale=pw(0), bias=pw(3)), reads=[UPb[s], PVFb], writes=[UCb[s2]])
                for j in (1, 2):
                    V(lambda h, j=j: h.scalar_tensor_tensor(out=uc, in0=uin(j), scalar=pw(j), in1=uc, op0=ALU.mult, op1=ALU.add),
                      reads=[UPb[s], PVFb, UCb[s2]], writes=[UCb[s2]])
                A(lambda h: h.activation(out=GU[:, s2, :], in_=UC[:, s2, :], func=AF.Gelu_apprx_tanh), reads=[UCb[s2]], writes=[GUb[s2]])
                V(lambda h: h.tensor_tensor(out=HID[:, cc, 0:T], in0=bv.t[:, 0:T], in1=GU[:, s2, :], op=ALU.mult),
                  reads=[bv.b, GUb[s2]], writes=[HIDb[cc]])

            def evacU(bi, banks):
                for jj in range(2):
                    chunk(2 * bi + jj, banks[jj], banks[2 + jj])

            linear_fm(f"up{l}", Wu, [[(256 * i, 256), (D_FF + 256 * i, 256)] for i in range(NFC // 2)], xb_rhs(T), T, evacU)
            if not isP:
                Dq("sp", ffc_s[l], FO[:], reads=[FOb])
            bm, bq = proj_res(tl, f"down{l}", Wd, [(HID[:, kc, 0:T], HIDb[kc]) for kc in range(NFC)])
        layer_norm(tl, l, 1, bm, bq)

    def final_outputs():
        for e in range(2):
            if 2 * e < cfg.nl:
                for hd in range(4):
                    Dq("sp", ret_p[e, hd].rearrange("(c p) v -> p c v", p=128), RS[e][:, hd, :].rearrange("p (c v) -> p c v", c=2), reads=[RSb[e][hd]])
                with Phase() as ph:
                    o1, o1b = ph.sb("fo1", [3, 1024], F32)
                    o2, o2b = ph.sb("fo2", [1, 1024], F32)
                    for q in range(2):
                        bk = ps_alloc()
                        MM([mmf(bk.t[0:3, j * 128:(j + 1) * 128], RGT[e][0][:, 4 * q + j, :], IDENT, j == 0, j == 3) for j in range(4)],
                           reads=[RGT[e][1], CTb], writes=[bk.b])
                        A(lambda h, bk=bk, q=q: h.activation(out=o1[:, q * 512:(q + 1) * 512], in_=bk.t[0:3, :], func=AF.Copy), reads=[bk.b], writes=[o1b])
                        ps_free(bk)
                        bk = ps_alloc()
                        MM([mmf(bk.t[0:1, j * 128:(j + 1) * 128], RGH[e][0][:, 4 * q + j:4 * q + j + 1], IDENT, j == 0, j == 3) for j in range(4)],
                           reads=[RGH[e][1], CTb], writes=[bk.b])
                        A(lambda h, bk=bk, q=q: h.activation(out=o2[:, q * 512:(q + 1) * 512], in_=bk.t[0:1, :], func=AF.Copy), reads=[bk.b], writes=[o2b])
                        ps_free(bk)
                    Dq("sp", rgc_p[e], o1[:], reads=[o1b])
                    Dq("sp", rgh_p[e:e + 1, :], o2[:], reads=[o2b])
            if 2 * e + 1 < cfg.nl:
                Dq("sp", hg_p[e].rearrange("h k v -> k h v"), HS[e][:], reads=HSb[e])
        for l in range(cfg.nl):
            with Phase() as ph:
                o3, o3b = ph.sb("fo3", [2, D_FF], F32)
                for q in range(NFC // 4):
                    bk = ps_alloc()
                    MM([mmf(bk.t[0:2, j * 128:(j + 1) * 128], FTL[l][0][:, 4 * q + j, :], IDENT, j == 0, j == 3) for j in range(4)],
                       reads=[FTL[l][1], CTb], writes=[bk.b])
                    A(lambda h, bk=bk, q=q: h.activation(out=o3[:, q * 512:(q + 1) * 512], in_=bk.t[0:2, :], func=AF.Copy), reads=[bk.b], writes=[o3b])
                    ps_free(bk)
                Dq("sp", ffc_p[l], o3[:], reads=[o3b])

    setup()
    for ti, tn in enumerate(cfg.tiles):
        cur_tile[0] = ti
        tl = Tile("p", int(tn[1:])) if tn[0] == "p" else Tile("s", 0)
        if cfg.on("load"):
            load_x(tl)
        for l in range(cfg.nl):
            with Phase() as phm:
                MIX, MIXb = phm.sb("MIX", [128, 16, tl.T], BF16, nb=16)
                if l % 2 == 0:
                    if cfg.on("mix"):
                        even_mixer(tl, l // 2, MIX, MIXb)
                    Wo = W["ev_w_out"][l // 2]
                else:
                    if cfg.on("mix"):
                        odd_mixer(tl, l // 2, MIX, MIXb)
                    Wo = W["od_w_out"][l // 2]
                bm, bq = proj_res(tl, f"out{l}", Wo, [(MIX[:, c, 0:tl.T], MIXb[c]) for c in range(16)])
            layer_norm(tl, l, 0, bm, bq)
            if cfg.on("ffn"):
                ffn(tl, l)
        if cfg.on("store"):
            store_y(tl)
    if cfg.on("final"):
        final_outputs()
    k.finish("sp")
    es.close()
    return nc, (k.n_inst, k.n_wait)


_CACHE = {}


def _get_program(cfg):
    key = cfg.key()
    if key not in _CACHE:
        rot, ct, cdec_p, cdec_s = _host_tables()
        nc, stats = build_program(cfg, cdec_p, cdec_s)
        _CACHE[key] = (nc, rot, ct, stats)
    return _CACHE[key]


def make_in_maps(inp, cores, rot, ct):
    f = lambda a: np.ascontiguousarray(np.asarray(a, dtype=np.float32))
    pv_even = np.stack([np.concatenate([inp["ev_rg_conv_w"][e], inp["ev_rg_conv_b"][e][None], inp["ev_rg_ba"][e][None],
                                        inp["ev_rg_bx"][e][None], inp["ev_rg_lambda"][e][None]], axis=0) for e in range(2)])
    pv_ln = np.concatenate([np.asarray(inp["ln_g"]).reshape(8, D_MODEL), np.asarray(inp["ln_b"]).reshape(8, D_MODEL)], axis=0)
    pv_ffn = np.concatenate([np.concatenate([inp["ffn_conv_w"][l], inp["ffn_conv_b"][l][None]], axis=0) for l in range(4)], axis=0)
    pv_od = np.concatenate([np.asarray(inp["od_lb_logits"]), np.tile(np.asarray(inp["od_norm_g"]), (1, 16))], axis=0)
    shared = {
        "ev_w_in": f(inp["ev_w_in"]), "ev_w_out": f(inp["ev_w_out"]), "ev_rg_wa": f(inp["ev_rg_wa"]), "ev_rg_wx": f(inp["ev_rg_wx"]),
        "od_w_in": f(inp["od_w_in"]), "od_w_out": f(inp["od_w_out"]), "ffn_w_up": f(inp["ffn_w_up"]), "ffn_w_down": f(inp["ffn_w_down"]),
        "pv_even": f(pv_even), "pv_ln": f(pv_ln), "pv_ffn": f(pv_ffn), "pv_od": f(pv_od), "rot": rot, "ctab": ct,
    }
    maps = []
    for c in cores:
        sl = slice(NS * c, NS * (c + 1))
        m = dict(shared)
        m["xp"] = f(inp["x_prompt"][c % 4])
        m["xs"] = f(np.asarray(inp["x_sample"][sl]).reshape(TS, D_MODEL))
        m["s_ret"] = f(np.asarray(inp["state_ret"])[:, sl])
        m["s_rgh"] = f(np.asarray(inp["state_rglru_h"])[:, sl])
        m["s_rgc"] = f(np.asarray(inp["state_rglru_conv"])[:, sl].reshape(2, NS * 3, 1024))
        m["s_hg"] = f(np.asarray(inp["state_hgrn"])[:, sl])
        m["s_ffc"] = f(np.asarray(inp["state_ffn_conv"])[:, sl].reshape(4, NS * 2, D_FF))
        maps.append(m)
    return maps


def assemble(results, cores):
    f32 = np.float32
    y_prompt = np.zeros((4, SEQ, D_MODEL), f32)
    y_sample = np.zeros((DEC_BATCH, DEC_SEQ, D_MODEL), f32)
    rp = np.zeros((2, 4, 4, 256, 256), f32)
    rs = np.zeros((2, DEC_BATCH, 4, 256, 256), f32)
    hp = np.zeros((2, 4, 1024), f32)
    hs = np.zeros((2, DEC_BATCH, 1024), f32)
    cp = np.zeros((2, 4, 3, 1024), f32)
    cs = np.zeros((2, DEC_BATCH, 3, 1024), f32)
    gp = np.zeros((2, 4, 16, 128, 128), f32)
    gs = np.zeros((2, DEC_BATCH, 16, 128, 128), f32)
    fp = np.zeros((4, 4, 2, D_FF), f32)
    fs = np.zeros((4, DEC_BATCH, 2, D_FF), f32)
    for c, r in zip(cores, results):
        sl = slice(NS * c, NS * (c + 1))
        if c < 4:
            y_prompt[c] = r["y_p"]
            rp[:, c] = r["ret_p"]
            hp[:, c] = r["rgh_p"]
            cp[:, c] = r["rgc_p"]
            gp[:, c] = r["hg_p"]
            fp[:, c] = r["ffc_p"]
        y_sample[sl] = r["y_s"].reshape(NS, DEC_SEQ, D_MODEL)
        rs[:, sl] = r["ret_s"]
        hs[:, sl] = r["rgh_s"]
        cs[:, sl] = r["rgc_s"].reshape(2, NS, 3, 1024)
        gs[:, sl] = r["hg_s"]
        fs[:, sl] = r["ffc_s"].reshape(4, NS, 2, D_FF)
    return (y_prompt, y_sample, rp, rs, hp, hs, cp, cs, gp, gs, fp, fs)


def kernel(**inputs):
    cfg = Cfg()
    nc, rot, ct, _ = _get_program(cfg)
    cores = list(range(8))
    maps = make_in_maps(inputs, cores, rot, ct)
    res = run_bass_kernel_spmd(nc, maps, core_ids=cores)
    return assemble(res.results, cores)
```
